# Optimizing a Trainium2 kernel written in Bass

```python
import math
import jax, jax.numpy as jnp
from jax import lax
import numpy as np

D_MODEL = 1024
BATCH = 16
SEQ = 2048
DEPTH = 1
DEC_BATCH = 1
DEC_SEQ = 16384
PAST_LEN = 128

EPS = 1e-6
POOL_WINDOWS = (2, 4, 8, 16)
POOL_GROUPS = len(POOL_WINDOWS)
POOL_WIDTH = D_MODEL
POOL_GDIM = POOL_WIDTH // POOL_GROUPS
FOUR_GROUPS = 4
FOUR_WIDTH = D_MODEL // 2
FOUR_GDIM = FOUR_WIDTH // FOUR_GROUPS
ATTN_HEADS = 4
ATTN_WIDTH = D_MODEL // 2
HEAD_DIM = ATTN_WIDTH // ATTN_HEADS
N_MEM = 256
N_BRANCH = 3
IN_SPLITS = (POOL_WIDTH, POOL_WIDTH, FOUR_WIDTH, FOUR_WIDTH, ATTN_WIDTH, ATTN_WIDTH, N_BRANCH * D_MODEL)
IN_WIDTH = sum(IN_SPLITS)

kernel_name = "gated_pool_fourier_memxattn_encoder"


def rmsnorm(x, g):
    xf = x.astype(jnp.float32)
    r = lax.rsqrt(jnp.mean(xf * xf, axis=-1, keepdims=True) + EPS)
    return (xf * r * g.astype(jnp.float32)).astype(x.dtype)


def split_cols(z):
    offs = np.cumsum(IN_SPLITS)[:-1]
    return jnp.split(z, offs, axis=-1)


def pool_mixer(u, w_grp, scale):
    B, S, _ = u.shape
    ug = u.reshape(B, S, POOL_GROUPS, POOL_GDIM).astype(jnp.float32)
    cs = jnp.concatenate([jnp.zeros((B, 1, POOL_GROUPS, POOL_GDIM), jnp.float32), jnp.cumsum(ug, axis=1)], axis=1)
    t = np.arange(S)
    outs = []
    for g, w in enumerate(POOL_WINDOWS):
        half = w // 2
        lo = np.clip(t - half, 0, S)
        hi = np.clip(t + half, 0, S)
        cnt = jnp.asarray((hi - lo).astype(np.float32))[None, :, None]
        csg = cs[:, :, g]
        win_sum = jnp.take(csg, jnp.asarray(hi), axis=1) - jnp.take(csg, jnp.asarray(lo), axis=1)
        outs.append(win_sum / cnt - ug[:, :, g])
    p = jnp.stack(outs, axis=2).astype(u.dtype)
    y = jnp.einsum('bsgc,gcd->bsgd', p, w_grp).reshape(B, S, POOL_WIDTH)
    return y * scale


def fourier_mixer(u, w_grp):
    B, S, _ = u.shape
    uh = u.reshape(B, S, FOUR_GROUPS, FOUR_GDIM).astype(jnp.float32)
    f = jnp.fft.fftn(uh, axes=(1, 3), norm='ortho').real.astype(u.dtype)
    return jnp.einsum('bsgc,gcd->bsgd', f, w_grp).reshape(B, S, FOUR_WIDTH)


def memory_xattn(q, mem_n, w_kv):
    B, S, _ = q.shape
    kv = mem_n @ w_kv
    k, v = jnp.split(kv, 2, axis=-1)
    qh = q.reshape(B, S, ATTN_HEADS, HEAD_DIM)
    kh = k.reshape(B, N_MEM, ATTN_HEADS, HEAD_DIM)
    vh = v.reshape(B, N_MEM, ATTN_HEADS, HEAD_DIM)
    s = jnp.einsum('bshd,bmhd->bhsm', qh, kh).astype(jnp.float32) * (1.0 / math.sqrt(HEAD_DIM))
    p = jax.nn.softmax(s, axis=-1).astype(vh.dtype)
    o = jnp.einsum('bhsm,bmhd->bshd', p, vh)
    return o.reshape(B, S, ATTN_WIDTH)


def trunk(x, mem, norm_in, norm_mem, w_in, w_pool_grp, pool_scale, w_four_grp, w_kv,
          w_pool_out, w_four_out, w_attn_out, b_gate, w_o, norm_f):
    for l in range(DEPTH):
        xn = rmsnorm(x, norm_in[l])
        mn = rmsnorm(mem, norm_mem[l])
        pv, pg, fv, fg, q, ag, mg = split_cols(xn @ w_in[l])
        ya = (pool_mixer(pv, w_pool_grp[l], pool_scale[l]) * jax.nn.silu(pg)) @ w_pool_out[l]
        yb = (fourier_mixer(fv, w_four_grp[l]) * jax.nn.silu(fg)) @ w_four_out[l]
        yc = (memory_xattn(q, mn, w_kv[l]) * jax.nn.silu(ag)) @ w_attn_out[l]
        ga, gb, gc = jnp.split(jax.nn.sigmoid(mg + b_gate[l]), N_BRANCH, axis=-1)
        merged = ga * ya + gb * yb + gc * yc
        x = x + merged @ w_o[l]
    return rmsnorm(x, norm_f)


def setup_inputs(seed: int = 0) -> dict:
    key = jax.random.key(seed)
    ks = jax.random.split(key, 20)
    f32 = jnp.float32
    nrm = lambda k, shp, s: jax.random.normal(k, shp, f32) * s
    return {
        'x_prompt': nrm(ks[0], (BATCH, SEQ, D_MODEL), 1.0),
        'x_sample': nrm(ks[1], (DEC_BATCH, DEC_SEQ, D_MODEL), 1.0),
        'mem_prompt': nrm(ks[2], (BATCH, N_MEM, D_MODEL), 1.0),
        'mem_sample': nrm(ks[3], (DEC_BATCH, N_MEM, D_MODEL), 1.0),
        'norm_in': 1.0 + nrm(ks[4], (DEPTH, D_MODEL), 0.05),
        'norm_mem': 1.0 + nrm(ks[5], (DEPTH, D_MODEL), 0.05),
        'w_in': nrm(ks[6], (DEPTH, D_MODEL, IN_WIDTH), D_MODEL ** -0.5),
        'w_pool_grp': nrm(ks[7], (DEPTH, POOL_GROUPS, POOL_GDIM, POOL_GDIM), POOL_GDIM ** -0.5),
        'pool_scale': 1.0 + nrm(ks[8], (DEPTH, POOL_WIDTH), 0.05),
        'w_four_grp': nrm(ks[9], (DEPTH, FOUR_GROUPS, FOUR_GDIM, FOUR_GDIM), FOUR_GDIM ** -0.5),
        'w_kv': nrm(ks[10], (DEPTH, D_MODEL, 2 * ATTN_WIDTH), D_MODEL ** -0.5),
        'w_pool_out': nrm(ks[11], (DEPTH, POOL_WIDTH, D_MODEL), POOL_WIDTH ** -0.5),
        'w_four_out': nrm(ks[12], (DEPTH, FOUR_WIDTH, D_MODEL), FOUR_WIDTH ** -0.5),
        'w_attn_out': nrm(ks[13], (DEPTH, ATTN_WIDTH, D_MODEL), ATTN_WIDTH ** -0.5),
        'b_gate': nrm(ks[14], (DEPTH, N_BRANCH * D_MODEL), 0.1),
        'w_o': nrm(ks[15], (DEPTH, D_MODEL, D_MODEL), D_MODEL ** -0.5),
        'norm_f': 1.0 + nrm(ks[16], (D_MODEL,), 0.05),
    }


def reference(x_prompt, x_sample, mem_prompt, mem_sample, norm_in, norm_mem, w_in, w_pool_grp,
              pool_scale, w_four_grp, w_kv, w_pool_out, w_four_out, w_attn_out, b_gate, w_o, norm_f):
    y_prompt = trunk(x_prompt, mem_prompt, norm_in, norm_mem, w_in, w_pool_grp, pool_scale, w_four_grp,
                     w_kv, w_pool_out, w_four_out, w_attn_out, b_gate, w_o, norm_f)
    y_sample = trunk(x_sample, mem_sample, norm_in, norm_mem, w_in, w_pool_grp, pool_scale, w_four_grp,
                     w_kv, w_pool_out, w_four_out, w_attn_out, b_gate, w_o, norm_f)
    return (y_prompt, y_sample)
```

```python
import math
from contextlib import ExitStack

import numpy as np
import ml_dtypes

import concourse.bass as bass
import concourse.mybir as mybir
from concourse.bass_utils import run_bass_kernel_spmd

F32 = mybir.dt.float32
BF16 = mybir.dt.bfloat16
AF = mybir.ActivationFunctionType
ALU = mybir.AluOpType

NCORES = 8
D = 1024
SEG = 2048
NT = 16
CH = 512
EPS = 1e-6
ENGS = ("pe", "act", "dve", "pool", "sp")


class _Op:
    __slots__ = ("eng", "fn", "deps", "dma", "semkey", "need_inc", "val", "waits", "grp_total", "inc")

    def __init__(self, eng, fn, deps, dma, semkey, grp_total, inc=16):
        self.eng, self.fn, self.deps, self.dma, self.semkey = eng, fn, deps, dma, semkey
        self.inc = inc
        self.need_inc = False
        self.val = None
        self.waits = []
        self.grp_total = grp_total


class Sched:
    def __init__(self, nc):
        self.nc = nc
        self.ops = []
        self.last_w = {}
        self.reads = {}
        self.fence_ops = set()
        self.last_eng = {}
        self.open_dma = []
        self.rgn_keys = set()

    def op(self, eng, fn, reads=(), writes=(), dma=False, semkey=None, grp_total=False, inc=16, nofence=False):
        if self.rgn_keys and (self.rgn_keys & (set(reads) | set(writes))) and "RGN" not in writes:
            reads = list(reads) + ["RGN"]
        deps = set() if nofence else set(self.fence_ops)
        for k in reads:
            w = self.last_w.get(k)
            if w is not None:
                deps.add(w)
        for k in writes:
            w = self.last_w.get(k)
            if w is not None:
                deps.add(w)
            deps.update(self.reads.get(k, ()))
        o = _Op(eng, fn, deps, dma, semkey, grp_total, inc)
        self.ops.append(o)
        for k in reads:
            self.reads.setdefault(k, []).append(o)
        for k in writes:
            self.last_w[k] = o
            self.reads[k] = []
        if dma:
            self.open_dma.append(o)
        else:
            self.last_eng[eng] = o
        return o

    def fence(self):
        self.fence_ops = set(self.last_eng.values()) | set(self.open_dma)
        self.open_dma = []

    def finalize(self):
        for o in self.ops:
            o.deps = {d for d in o.deps if not (o.eng == "pe" and d.eng == "pe" and not d.dma and not o.dma)}
            for d in o.deps:
                if not d.dma:
                    d.need_inc = True
        eng_cnt = {e: 0 for e in ENGS}
        dma_cnt = {}
        for o in self.ops:
            if o.dma:
                dma_cnt[o.semkey] = dma_cnt.get(o.semkey, 0) + o.inc
                o.val = dma_cnt[o.semkey]
            elif o.need_inc:
                eng_cnt[o.eng] += 1
                o.val = eng_cnt[o.eng]
        for o in self.ops:
            if o.dma and o.grp_total:
                o.val = dma_cnt[o.semkey]
        known = {e: {} for e in ENGS}
        for o in self.ops:
            need = {}
            for d in o.deps:
                key = ("dma", d.semkey) if d.dma else ("eng", d.eng)
                if need.get(key, 0) < d.val:
                    need[key] = d.val
            kn = known[o.eng]
            for key, v in need.items():
                if kn.get(key, 0) >= v:
                    continue
                kn[key] = v
                o.waits.append((key, v))
        self.dma_keys = sorted(dma_cnt.keys(), key=str)
        self.dma_cnt = dma_cnt

    def emit(self, final_keys):
        nc = self.nc
        with ExitStack() as st:
            esem = {e: st.enter_context(nc.semaphore("s_" + e)) for e in ENGS}
            dsem = {k: st.enter_context(nc.semaphore("d%d" % i)) for i, k in enumerate(self.dma_keys)}
            block = st.enter_context(nc.Block())
            by_eng = {e: [o for o in self.ops if o.eng == e] for e in ENGS}

            def run(e, eng):
                for o in by_eng[e]:
                    for key, v in o.waits:
                        eng.wait_ge(dsem[key[1]] if key[0] == "dma" else esem[key[1]], v)
                    inst = o.fn(eng)
                    if o.dma:
                        inst.then_inc(dsem[o.semkey], o.inc)
                    elif o.need_inc:
                        inst.then_inc(esem[e], 1)
                for k in final_keys.get(e, ()):
                    eng.wait_ge(dsem[k], self.dma_cnt[k])

            block.tensor(lambda eng: run("pe", eng))
            block.scalar(lambda eng: run("act", eng))
            block.vector(lambda eng: run("dve", eng))
            block.gpsimd(lambda eng: run("pool", eng))
            block.sync(lambda eng: run("sp", eng))


def build_program():
    nc = bass.Bass("TRN2", target_bir_lowering=False)
    S = Sched(nc)

    def din(name, shape, dt=F32):
        return nc.dram_tensor(name, list(shape), dt, kind="ExternalInput")

    xs = din("xs", [3, SEG, D])
    xh = din("xh", [3, 16, D])
    mem = din("mem", [3, 256, D])
    xfull = din("xfull", [16384, D])
    t_kbo = din("t_kbo", [128, 64], BF16)
    w_in = din("w_in", [D, 7168])
    w_kv = din("w_kv", [D, 1024])
    w_po = din("w_po", [D, 1024])
    w_fo = din("w_fo", [512, 1024])
    w_ao = din("w_ao", [512, 1024])
    w_o = din("w_o", [D, 1024])
    w_pg = din("w_pg", [1024, 256])
    w_fg = din("w_fg", [128, 4, 128])
    cols = din("cols", [128, 48])
    nf_row = din("nf_row", [1, D])
    icnt = din("icnt", [1, 3 * 4 * 16])
    t_ident = din("t_ident", [128, 128], BF16)
    t_dfta = din("t_dfta", [128, 256], BF16)
    t_twid = din("t_twid", [128, 2 * 2 * 256])
    t_kb = din("t_kb", [128, 2 * 2 * 256], BF16)
    t_cs = din("t_cs", [128, 256])
    yout = nc.dram_tensor("yout", [3, SEG, D], F32, kind="ExternalOutput")

    wb_in = nc.dram_tensor("wb_in", [28, 128, 2048], BF16)
    wb_kv = nc.dram_tensor("wb_kv", [4, 128, 2048], BF16)
    wb_po = nc.dram_tensor("wb_po", [4, 128, 2048], BF16)
    wb_fo = nc.dram_tensor("wb_fo", [4, 128, 1024], BF16)
    wb_ao = nc.dram_tensor("wb_ao", [4, 128, 1024], BF16)
    wb_o = nc.dram_tensor("wb_o", [4, 128, 2048], BF16)
    wb_pg = nc.dram_tensor("wb_pg", [1024, 256], BF16)
    fvS = nc.dram_tensor("fvS", [2, SEG, 512], BF16)
    G1 = nc.dram_tensor("G1", [16384, 512], BF16)

    st = ExitStack()
    with st:
        def sb(name, shape, dt):
            return st.enter_context(nc.sbuf_tensor(name, list(shape), dt))

        ident = sb("ident", [128, 128], BF16)
        ones = sb("ones", [128, 128], BF16)
        dfta = sb("dfta", [128, 256], BF16)
        twid2 = sb("twid", [128, 1024], F32)
        twid = twid2[:, :].rearrange("p (a b c d) -> p a b c d", a=2, b=2, c=2)
        kb2 = sb("kb", [128, 1024], BF16)
        kb = kb2[:, :].rearrange("p (a b c) -> p a b c", a=2, b=2)
        kbo2 = sb("kbo", [128, 64], BF16)
        kbo = kbo2[:, :].rearrange("p (a b) -> p a b", a=2)
        csm2 = sb("csm", [128, 256], F32)
        csm = csm2[:, :].rearrange("p (a b) -> p a b", a=2)
        wfg2 = sb("wfg", [128, 512], F32)
        wfg = wfg2[:, :].rearrange("p (a b) -> p a b", a=4)
        fold = sb("fold", [128, 2, 4, 2, 128], BF16)
        colsb = sb("colsb", [128, 48], F32)
        nfr = sb("nfr", [128, D], F32)
        icn2 = sb("icn", [128, 192], F32)
        icn = icn2[:, :].rearrange("p (a b c) -> p a b c", a=3, b=4)
        wpg = sb("wpg", [128, 8, 256], BF16)
        neghalf = sb("neghalf", [128, 1], F32)
        xnT = sb("xnT", [128, 8, SEG + 16], BF16)
        GT = sb("GT", [128, 4, SEG], BF16)
        KT = sb("KT", [128, 4, 256], BF16)
        Vv = sb("Vv", [128, 2, 512], BF16)
        mnT = sb("mnT", [128, 8, 256], BF16)
        NSLOT = 9
        wslot = [sb("wslot%d" % i, [128, 2048], BF16) for i in range(NSLOT)]
        xin = [sb("xin%d" % i, [128, D], F32) for i in range(2)]
        hout = [sb("hout%d" % i, [128, D], F32) for i in range(2)]
        stat = sb("stat", [128, 8, 24], F32)
        ARENA = 37 * 1024
        arena = sb("arena", [128, ARENA], BF16)
        apos = [0]

        def carve(shape, dt):
            n = int(np.prod(shape))
            nb = n * (2 if dt == F32 else 1)
            a = arena[:, apos[0]:apos[0] + nb]
            apos[0] += nb
            assert apos[0] <= ARENA, apos[0]
            if dt == F32:
                a = a.bitcast(F32)
            if len(shape) == 2:
                return a.rearrange("p (a b) -> p a b", b=shape[1])
            if len(shape) == 3:
                return a.rearrange("p (a b c) -> p a b c", b=shape[1], c=shape[2])
            return a

        apos[0] = 0
        Dt = carve([16 * 512], BF16)
        Fsb = carve([2 * 16 * 512], BF16)
        w4b = [carve([4, 128], BF16) for _ in range(3)]
        asb = [carve([256], F32) for _ in range(3)]
        posX = apos[0]
        Dt2 = carve([16 * 512], BF16)
        apos[0] = posX
        xsb = [carve([D], BF16) for _ in range(2)]
        fvt = [carve([512], BF16) for _ in range(2)]
        FTt = carve([8, 512], BF16)
        endA = apos[0]
        apos[0] = 16 * 512
        xinP = list(xin) + [carve([D], F32) for _ in range(4)]
        xsbP = list(xsb) + [carve([D], BF16) for _ in range(2)]
        sqjunk = carve([D], BF16)
        apos[0] = 0
        gaT = carve([8, CH], BF16)
        gcT = carve([4, CH], BF16)
        gbT = carve([4, CH], BF16)
        gate = [carve([3, CH], BF16) for _ in range(2)]
        m1 = carve([CH], F32)
        m2 = carve([CH], F32)
        m3 = carve([CH], F32)
        mT = carve([8, CH], BF16)
        posR = apos[0]
        pvh = carve([8, CH + 16], BF16)
        tA = carve([2, CH + 16], F32)
        tB = carve([2, CH + 16], F32)
        tC = carve([2, CH + 16], F32)
        pT = carve([8, CH], BF16)
        pgs = carve([8, CH], BF16)
        endB1 = apos[0]
        apos[0] = posR
        qT = carve([4, CH], BF16)
        ags = carve([4, CH], BF16)
        Ee = [carve([2, CH], BF16) for _ in range(2)]
        rden = carve([CH], F32)
        otmp = carve([CH], F32)
        fgs = carve([4, CH], BF16)
        endB = max(endB1, apos[0])

        psW = [st.enter_context(nc.psum_tensor("psW%d" % i, [128, 1024], F32)) for i in range(4)]
        psF = [psW[i // 2][:, (i % 2) * 512:(i % 2 + 1) * 512] for i in range(8)]
        psT = [psW[2], psW[3]]
        rr = {"F": 0, "T": 0, "w": 0, "nF": 8}

        def nextF():
            i = rr["F"] % rr["nF"]
            rr["F"] = (i + 1) % rr["nF"]
            return psF[i], "psF%d" % i

        class _Wide:
            def __init__(self, t, keys):
                self.t, self.keys = t, keys

        def nextT():
            i = rr["T"]
            rr["T"] = 1 - i
            return psT[i], ["psF%d" % (4 + 2 * i), "psF%d" % (5 + 2 * i)]

        _pid = {}

        def getpid(e):
            k = id(e)
            if k not in _pid:
                _pid[k] = e.partition_id()
            return _pid[k]

        def dma(eng, out, in_, reads, writes, semkey, grp_total=False, nofence=False):
            return S.op(eng, lambda e, out=out, in_=in_: e.dma_start(out=out, in_=in_),
                        reads=reads, writes=writes, dma=True, semkey=semkey, grp_total=grp_total, nofence=nofence)

        def mm(out, lhsT, rhs, start, stop, reads, writes):
            return S.op("pe", lambda e, out=out, lhsT=lhsT, rhs=rhs, start=start, stop=stop:
                        e.matmul(out, lhsT=lhsT, rhs=rhs, start=start, stop=stop), reads=reads, writes=writes)

        def act(out, in_, func, reads, writes, bias=None, scale=None, accum_out=None):
            kw = {}
            if bias is not None:
                kw["bias"] = bias
            if scale is not None:
                kw["scale"] = scale
            if accum_out is not None:
                kw["accum_out"] = accum_out
            return S.op("act", lambda e, out=out, in_=in_, func=func, kw=kw: e.activation(out=out, in_=in_, func=func, **kw),
                        reads=reads, writes=writes)

        def tt(eng, out, in0, in1, op, reads, writes):
            return S.op(eng, lambda e, out=out, in0=in0, in1=in1, op=op: e.tensor_tensor(out=out, in0=in0, in1=in1, op=op),
                        reads=reads, writes=writes)

        def ts(eng, out, in0, s1, s2, op0, op1, reads, writes):
            if s2 is None:
                return S.op(eng, lambda e, out=out, in0=in0, s1=s1, op0=op0: e.tensor_scalar(out=out, in0=in0, scalar1=s1, scalar2=None, op0=op0),
                            reads=reads, writes=writes)
            return S.op(eng, lambda e, out=out, in0=in0, s1=s1, s2=s2, op0=op0, op1=op1:
                        e.tensor_scalar(out=out, in0=in0, scalar1=s1, scalar2=s2, op0=op0, op1=op1), reads=reads, writes=writes)

        def stt(eng, out, in0, scalar, in1, op0, op1, reads, writes):
            return S.op(eng, lambda e, out=out, in0=in0, scalar=scalar, in1=in1, op0=op0, op1=op1:
                        e.scalar_tensor_tensor(out=out, in0=in0, scalar=scalar, in1=in1, op0=op0, op1=op1), reads=reads, writes=writes)

        def cp(eng, out, in_, reads, writes):
            if eng == "act":
                return act(out, in_, AF.Copy, reads, writes)
            return S.op(eng, lambda e, out=out, in_=in_: e.tensor_copy(out=out, in_=in_), reads=reads, writes=writes)

        def load_w(dram_ap, shape3, dep="wcast"):
            i = rr["w"]
            rr["w"] = (i + 1) % NSLOT
            n = shape3[0] * shape3[1]
            v = wslot[i][:, 0:n].rearrange("p (k c) -> p k c", c=shape3[1])
            key = "wslot%d" % i
            dma("sp", v, dram_ap, reads=[dep], writes=[key], semkey=key, nofence=True)
            return v, key

        def load_l(li, dram_ap, shape3):
            n = shape3[0] * shape3[1]
            v = lslot[li][:, 0:n].rearrange("p (k c) -> p k c", c=shape3[1])
            key = "lslot%d" % li
            dma("sp", v, dram_ap, reads=["wcast"], writes=[key], semkey=key)
            return v, key

        def w_cols(wb, c0, ncol, kchunks=8):
            assert ncol == 256 and c0 % 256 == 0, (c0, ncol)
            return wb[c0 // 256, :, :].rearrange("p (k c) -> p k c", k=kchunks), (kchunks, ncol)

        def cast_blk(dst, src, cb, kk):
            return (dst[cb, :, :].rearrange("p (k c) -> p k c", k=kk), src[:, cb * 256:(cb + 1) * 256].rearrange("(k p) c -> p k c", p=128))

        for ci, cb in enumerate((8, 9)):
            o_, i_ = cast_blk(wb_in, w_in, cb, 8)
            dma("pool", o_, i_, reads=[], writes=(["wcast_fv"] if ci == 1 else []), semkey="wcast_fv", grp_total=True)
        casts = [cast_blk(wb_in, w_in, cb, 8) for cb in range(28) if cb not in (8, 9)]
        casts += [cast_blk(wb_kv, w_kv, cb, 8) for cb in range(4)]
        casts += [cast_blk(wb_po, w_po, cb, 8) for cb in range(4)]
        casts += [cast_blk(wb_fo, w_fo, cb, 4) for cb in range(4)]
        casts += [cast_blk(wb_ao, w_ao, cb, 4) for cb in range(4)]
        casts += [cast_blk(wb_o, w_o, cb, 8) for cb in range(4)]
        casts += [(wb_pg[:, :], w_pg[:, :])]
        cast_state = {"i": 0}

        def issue_casts(n):
            for _ in range(n):
                ci = cast_state["i"]
                if ci >= len(casts):
                    return
                o_, i_ = casts[ci]
                dma("pool", o_, i_, reads=[], writes=(["wcast"] if ci == len(casts) - 1 else []), semkey="wcast", grp_total=True)
                cast_state["i"] = ci + 1
        cl = lambda out, in_, last=False: dma("sp", out, in_, reads=[], writes=(["const"] if last else []), semkey="const", grp_total=True)
        cl(ident[:, :], t_ident[:, :])
        cl(dfta[:, :], t_dfta[:, :])
        cl(twid2[:, :], t_twid[:, :])
        cl(kb2[:, :], t_kb[:, :])
        cl(kbo2[:, :], t_kbo[:, :])
        cl(csm2[:, :], t_cs[:, :])
        cl(wfg2[:, :], w_fg[:, :, :].rearrange("p a b -> p (a b)"))
        cl(colsb[:, :], cols[:, :])
        cl(nfr[:, :], nf_row[0:1, :].partition_broadcast(128))
        cl(icn2[:, :], icnt[0:1, :].partition_broadcast(128), last=True)
        S.op("pool", lambda e: e.memset(ones[:, :], 1.0), writes=["ones"])
        S.op("pool", lambda e: e.memset(neghalf[:, :], -0.5), writes=["neghalf"])
        for h in range(4):
            for ab in range(2):
                ps, pk = nextF()
                mm(ps[:, 0:128], csm[:, ab, :], wfg[:, h, :], True, True, reads=["const"], writes=[pk])
                for v, Sq in enumerate((2048, 16384)):
                    act(fold[:, v, h, ab, :], ps[:, 0:128], AF.Copy, reads=[pk], writes=["fold"], scale=1.0 / math.sqrt(Sq * 128.0))

        NCOL = lambda k: colsb[:, k:k + 1]

        def rstd_from_ss(ss_ap, ms_ap, y_ap, t_ap, key):
            ts("dve", ms_ap, ss_ap, 1.0 / D, EPS, ALU.mult, ALU.add, reads=[key + "ss"], writes=[key + "ms"])
            tt("pool", y_ap, ms_ap, neghalf[0:ms_ap.shape[0], :], ALU.pow,
               reads=[key + "ms", "neghalf"], writes=[key + "y"])
            for _ in range(2):
                tt("dve", t_ap, y_ap, y_ap, ALU.mult, reads=[key + "y"], writes=[key + "t"])
                tt("dve", t_ap, t_ap, ms_ap, ALU.mult, reads=[key + "t", key + "ms"], writes=[key + "t"])
                ts("dve", t_ap, t_ap, -0.5, 1.5, ALU.mult, ALU.add, reads=[key + "t"], writes=[key + "t"])
                tt("dve", y_ap, y_ap, t_ap, ALU.mult, reads=[key + "y", key + "t"], writes=[key + "y"])

        def tile_pipeline(items, extra=None):
            rr["nF"] = 4
            rr["F"] = 0
            n = len(items)

            def xk_(i):
                return "xinP%d" % (i % 6)

            def sa(i):
                it = items[i]
                sl, npart = i % 4, it["npart"]
                xt = xinP[i % 6]
                dma("sp", xt[0:npart, :], it["src"], reads=[], writes=[xk_(i)], semkey=xk_(i))
                S.op("pool", lambda e, sl=sl: e.memset(stat[:, sl, 0:1], 0.0), writes=["st%dss" % sl])
                act(sqjunk[0:npart, :], xt[0:npart, :], AF.Square, reads=[xk_(i)], writes=["sqjunk", "st%dss" % sl],
                    accum_out=stat[0:npart, sl, 0:1])

            def sb_(i):
                it = items[i]
                sl, npart = i % 4, it["npart"]
                ts("dve", stat[0:npart, sl, 1:2], stat[0:npart, sl, 0:1], 1.0 / D, EPS, ALU.mult, ALU.add, reads=["st%dss" % sl], writes=["st%dms" % sl])
                tt("pool", stat[0:npart, sl, 2:3], stat[0:npart, sl, 1:2], neghalf[0:npart, :], ALU.pow,
                   reads=["st%dms" % sl, "neghalf"], writes=["st%dy" % sl])

            def sc(i):
                it = items[i]
                sl, npart = i % 4, it["npart"]
                key = "st%d" % sl
                ms_ap, y_ap, t_ap = stat[0:npart, sl, 1:2], stat[0:npart, sl, 2:3], stat[0:npart, sl, 3:4]
                for _ in range(2):
                    tt("dve", t_ap, y_ap, y_ap, ALU.mult, reads=[key + "y"], writes=[key + "t"])
                    tt("dve", t_ap, t_ap, ms_ap, ALU.mult, reads=[key + "t", key + "ms"], writes=[key + "t"])
                    ts("dve", t_ap, t_ap, -0.5, 1.5, ALU.mult, ALU.add, reads=[key + "t"], writes=[key + "t"])
                    tt("dve", y_ap, y_ap, t_ap, ALU.mult, reads=[key + "y", key + "t"], writes=[key + "y"])

            def sd(i):
                it = items[i]
                sl, npart = i % 4, it["npart"]
                act(xsbP[sl][0:npart, :], xinP[i % 6][0:npart, :], AF.Copy, reads=[xk_(i), "st%dy" % sl], writes=["xsbP%d" % sl],
                    scale=stat[0:npart, sl, 2:3])

            def se(i):
                it = items[i]
                sl, npart = i % 4, it["npart"]
                pt, pk = nextT()
                for k in range(8):
                    S.op("pe", lambda e, pt=pt, k=k, sl=sl, npart=npart:
                         e.matmul(pt[:, k * 128:k * 128 + npart], lhsT=xsbP[sl][0:npart, k * 128:(k + 1) * 128], rhs=ident[0:npart, 0:npart], start=True, stop=True),
                         reads=["xsbP%d" % sl, "const"], writes=pk)
                it["dst"](pt, pk)

            def sf(i):
                if items[i].get("post") is not None:
                    items[i]["post"]()

            stages = [sa, sb_, sc, sd, se, sf]
            for i in range(n + len(stages) - 1):
                for d_, fn_ in enumerate(stages):
                    if 0 <= i - d_ < n:
                        fn_(i - d_)
                if extra is not None:
                    extra(i)

        def xn_dst_t(t):
            return lambda pt, pk: xn_dst(pt, pk, t)

        def xn_dst(pt, pk, t):
            src = pt[:, :].rearrange("p (k c) -> p k c", c=128)
            tt("dve", xnT[:, :, 8 + t * 128: 8 + (t + 1) * 128], src,
               colsb[:, 0:8].unsqueeze(2).to_broadcast([128, 8, 128]), ALU.mult,
               reads=pk + ["const"], writes=["xnT_t%d" % t])

        def xn_halo_dst(pt, pk, _=None):
            src = pt[:, :].rearrange("p (k c) -> p k c", c=128)
            nb = colsb[:, 0:8].unsqueeze(2).to_broadcast([128, 8, 8])
            tt("dve", xnT[:, :, 0:8], src[:, :, 0:8], nb, ALU.mult, reads=pk + ["const"], writes=["xnT_hl"])
            tt("dve", xnT[:, :, SEG + 8:SEG + 16], src[:, :, 8:16], nb, ALU.mult, reads=pk + ["const"], writes=["xnT_hr"])

        def mn_dst_fn(mt):
            def f(pt, pk, _=None):
                src = pt[:, :].rearrange("p (k c) -> p k c", c=128)
                tt("dve", mnT[:, :, mt * 128:(mt + 1) * 128], src,
                   colsb[:, 8:16].unsqueeze(2).to_broadcast([128, 8, 128]), ALU.mult, reads=pk + ["const"], writes=["mnT"])
            return f

        xrr = [0]

        def next_x():
            i = xrr[0]
            xrr[0] = 1 - i
            return i

        def fv_post(t, tq, wv, wk, store_fn):
            def f():
                ps, pk = nextF()
                for q in range(2):
                    for k in range(8):
                        mm(ps[:, q * 256:(q + 1) * 256], xnT[:, k, 8 + tq * 128:8 + (tq + 1) * 128], wv[q][:, k, :], k == 0, k == 7,
                           reads=["xnT_t%d" % tq, wk[q]], writes=[pk])
                fs = t % 2
                act(fvt[fs][:, :], ps[:, :], AF.Copy, reads=[pk], writes=["fvt%d" % fs])
                store_fn(t, fvt[fs], "fvt%d" % fs)
            return f

        def pass1(s, fv_dst_fn, aux_only=False):
            wv = wk = None
            if fv_dst_fn is not None:
                nat = [load_w(w_cols(wb_in, 2048 + q * 256, 256)[0], (8, 256)) for q in range(2)]
                wv, wk = [], []
                for qp in range(2):
                    i = rr["w"]
                    rr["w"] = (i + 1) % NSLOT
                    pv_ = wslot[i][:, 0:2048].rearrange("p (k c) -> p k c", c=256)
                    pkey = "wslot%d" % i
                    for sq in range(2):
                        o_ = pv_.rearrange("p k (c g) -> p k c g", g=64)[:, :, :, sq * 32:(sq + 1) * 32]
                        i_ = nat[sq][0].rearrange("p k (g c) -> p k c g", c=8)[:, :, 4 * qp:4 * qp + 4, :]
                        cp("pool", o_, i_, reads=[nat[sq][1]], writes=[pkey])
                    wv.append(pv_)
                    wk.append(pkey)
            items = [dict(src=xh[s, :, :], npart=16, dst=xn_halo_dst, post=None)]
            for t in range(NT if not aux_only else 0):
                items.append(dict(src=xs[s, t * 128:(t + 1) * 128, :], npart=128, dst=xn_dst_t(t),
                                  post=(fv_post(t, t, wv, wk, fv_dst_fn) if fv_dst_fn is not None else None)))
            for mt in range(2):
                items.append(dict(src=mem[s, mt * 128:(mt + 1) * 128, :], npart=128, dst=mn_dst_fn(mt), post=None))
            tile_pipeline(items)
            for hp in range(2):
                wv, wk = load_w(w_cols(wb_kv, hp * 256, 256)[0], (8, 256))
                for hh in range(2):
                    h = 2 * hp + hh
                    ps, pk = nextF()
                    for k in range(8):
                        mm(ps[:, 0:256], wv[:, k, hh * 128:(hh + 1) * 128], mnT[:, k, :], k == 0, k == 7, reads=["mnT", wk], writes=[pk])
                    act(KT[:, h, :], ps[:, 0:256], AF.Copy, reads=[pk], writes=["KT"])
            wq_ = [load_w(w_cols(wb_kv, 512 + q * 256, 256)[0], (8, 256)) for q in range(2)]
            for mt in range(2):
                ps, pk = nextF()
                for q in range(2):
                    for k in range(8):
                        mm(ps[:, q * 256:(q + 1) * 256], mnT[:, k, mt * 128:(mt + 1) * 128], wq_[q][0][:, k, :], k == 0, k == 7,
                           reads=["mnT", wq_[q][1]], writes=[pk])
                act(Vv[:, mt, :], ps[:, :], AF.Copy, reads=[pk], writes=["Vv"])

        def fourier_groups(v, lhs_fn, out_fn, kbr=None, nB=256, dkey="Dt"):
            rr["nF"] = 4
            rr["F"] = 0
            if kbr is None:
                kbr = (kb[:, v, 0, :], kb[:, v, 1, :])
            pa = {}

            def fa(g):
                ps, pk = nextF()
                mm(ps[:, 0:256], lhs_fn(g), dfta[:, :], True, True, reads=[dkey, "const"], writes=[pk])
                pa[g] = (ps, pk)

            def ft0(g):
                ps, pk = pa[g]
                act(asb[g % 3][:, :], ps[:, 0:256], AF.Copy, reads=[pk], writes=["asb%d" % (g % 3)])

            def ft1(g):
                a_, ak = asb[g % 3], "asb%d" % (g % 3)
                w4, k4 = w4b[g % 3], "w4b_%d" % (g % 3)
                A2 = a_[:, :].rearrange("p (a b) -> p a b", a=2)
                tt("dve", w4[:, 0:2, :], A2, twid[:, v, 0, :, :], ALU.mult, reads=[ak, "const"], writes=[k4 + "a"])
                tt("dve", w4[:, 2, :], a_[:, 0:128], twid[:, v, 1, 0, :], ALU.mult, reads=[ak, "const"], writes=[k4 + "c"])
                tt("pool", w4[:, 3, :], a_[:, 128:256], twid[:, v, 1, 1, :], ALU.mult, reads=[ak, "const"], writes=[k4 + "d"])

            def fb(g):
                w4, k4 = w4b[g % 3], "w4b_%d" % (g % 3)
                ps2, pk2 = nextF()
                mm(ps2[:, 0:nB], w4[:, 0, :], kbr[0], True, False, reads=[k4 + "a", "const"], writes=[pk2])
                mm(ps2[:, 0:nB], w4[:, 1, :], kbr[0], False, False, reads=[k4 + "a", "const"], writes=[pk2])
                mm(ps2[:, 0:nB], w4[:, 2, :], kbr[1], False, False, reads=[k4 + "c", "const"], writes=[pk2])
                mm(ps2[:, 0:nB], w4[:, 3, :], kbr[1], False, True, reads=[k4 + "d", "const"], writes=[pk2])
                out_fn(g, ps2, pk2)

            for i in range(64 + 3):
                if i < 64:
                    fa(i)
                if 0 <= i - 1 < 64:
                    ft0(i - 1)
                if 0 <= i - 2 < 64:
                    ft1(i - 2)
                if 0 <= i - 3 < 64:
                    fb(i - 3)

        def f_finish(v, tile_fn, t0):
            for tl in range(4):
                hap, fkeys = tile_fn(t0 + tl)
                pt, pk = nextT()
                for h in range(4):
                    for ri in range(2):
                        for (iap, p0, npp) in hap(ri, h):
                            S.op("pe", lambda e, pt=pt, h=h, ri=ri, iap=iap, p0=p0, npp=npp:
                                 e.matmul(pt[p0:p0 + npp, (h * 2 + ri) * 128:(h * 2 + ri + 1) * 128], lhsT=iap, rhs=ident[:, :], start=True, stop=True),
                                 reads=fkeys + ["const"], writes=pk)
                cp(("dve", "act")[tl % 2], FTt[:, :, tl * 128:(tl + 1) * 128], pt[:, :].rearrange("p (a c) -> p a c", c=128), reads=pk, writes=["FTt%d" % tl])
            for h in range(4):
                ps, pk = nextF()
                mm(ps[:, :], fold[:, v, h, 0, :], FTt[:, h * 2, :], True, False, reads=["fold"] + ["FTt%d" % i for i in range(4)], writes=[pk])
                mm(ps[:, :], fold[:, v, h, 1, :], FTt[:, h * 2 + 1, :], False, True, reads=["fold"] + ["FTt%d" % i for i in range(4)], writes=[pk])
                cp("act", GT[:, h, t0 * 128:(t0 + 4) * 128], ps[:, :], reads=[pk], writes=["GT_%d" % (t0 // 4)])

        Dv16 = Dt[:, :].rearrange("p (i c) -> p i c", c=512)
        Dv128 = Dt[:, :].rearrange("p (i c) -> p i c", c=64)
        Fp = Fsb[:, :].rearrange("p (r k c) -> p r k c", r=2, k=16)
        Fs = Fsb[:, :].rearrange("p (r k c) -> p r k c", r=2, k=128)

        def fourier_prompt(s):
            dma("sp", Dt[:, :], fvS[s, :, :].rearrange("(p i) c -> p (i c)", i=16), reads=["fvS%d_%d" % (s, t) for t in range(NT)], writes=["Dt"], semkey="Dt")

            def out_fn(g, ps2, pk2):
                cp("act", Fp[:, :, :, 8 * g:8 * g + 8], ps2[:, 0:256].rearrange("p (r k c) -> p r k c", r=2, k=16), reads=[pk2], writes=["Fsb"])

            fourier_groups(0, lambda g: Dt[:, :].rearrange("p (m g) -> p m g", g=64)[:, :, g], out_fn)
            for t0 in range(0, NT, 4):
                f_finish(0, lambda t: ((lambda ri, h, t=t: [(Fp[:, ri, t, h * 128:(h + 1) * 128], 0, 128)]), ["Fsb"]), t0)

        def sample_fv_all():
            wq_ = [load_w(w_cols(wb_in, 2048 + q * 256, 256)[0], (8, 256), dep="wcast_fv") for q in range(2)]
            wv, wk = [x_[0] for x_ in wq_], [x_[1] for x_ in wq_]

            def store(t, tile, key):
                dma("act", G1[t * 128:(t + 1) * 128, :], tile[:, :], reads=[key], writes=["G1_%d" % t], semkey="fvst%d" % (t % 2))

            items = []
            for t in range(128):
                tq = t % NT
                items.append(dict(src=xfull[t * 128:(t + 1) * 128, :], npart=128, dst=xn_dst_t(tq), post=fv_post(t, tq, wv, wk, store)))
            tile_pipeline(items, extra=lambda i: issue_casts(1) if (i >= 8 and i % 2 == 0) else None)
            issue_casts(1000)
            dma("sp", wpg[:, :, :], wb_pg[:, :].rearrange("(k p) c -> p k c", p=128), reads=["wcast"], writes=["wpg"], semkey="wpg")

        def fourier_sample_local():
            Fp64 = Fp[:, :, :, :].rearrange("p r k (b c) -> p r k b c", c=64)
            Dbuf = [(Dt, "Dt"), (Dt2, "Dt2")]

            def load_blk(b):
                db, dk = Dbuf[b % 2]
                g1v = G1[:, :].rearrange("(p i) c -> p i c", i=128)
                dbv = db[:, :].rearrange("p (i c) -> p i c", c=64)
                for i0 in range(0, 128, 32):
                    dma("sp", dbv[:, i0:i0 + 32, :], g1v[:, i0:i0 + 32, b * 64:(b + 1) * 64],
                        reads=["G1_%d" % t for t in range(128)], writes=[dk], semkey=dk)

            load_blk(0)
            for b in range(8):
                if b + 1 < 8:
                    load_blk(b + 1)
                db, dk = Dbuf[b % 2]
                dv = db[:, :].rearrange("p (i c) -> p i c", c=64)

                def out_fn(g, ps2, pk2, b=b):
                    cp("act", Fp[:, :, :, 64 * b + g], ps2[:, 0:32].rearrange("p (r k) -> p r k", r=2), reads=[pk2], writes=["Fsb"])

                fourier_groups(1, lambda g, dv=dv: dv[:, :, g], out_fn, kbr=(kbo[:, 0, :], kbo[:, 1, :]), nB=32, dkey=dk)
            S.fence()
            for t0 in range(0, NT, 4):
                f_finish(1, lambda t: ((lambda ri, h, t=t: [(Fp[:, ri, t, h * 128:(h + 1) * 128], 0, 128)]), ["Fsb"]), t0)

        WIN_OFF = {"pv": 0, "pg": 1024, "fv": 2048, "fg": 2560, "q": 3072, "ag": 3584, "mg": 4096}

        wpref = {}

        def proj_fm(col0, nchunk, c0, evac_fn, xkeys):
            done = 0
            while done < nchunk:
                n = min(2, nchunk - done)
                if (col0 + done * 128) in wpref:
                    wv, wk = wpref.pop(col0 + done * 128)
                else:
                    wv, wk = load_w(w_cols(wb_in, col0 + done * 128, n * 128)[0], (8, n * 128))
                for j in range(n):
                    ps, pk = nextF()
                    for k in range(8):
                        mm(ps[:, :], wv[:, k, j * 128:(j + 1) * 128], xnT[:, k, c0:c0 + CH], k == 0, k == 7, reads=xkeys + [wk], writes=[pk])
                    evac_fn(done + j, ps, pk, wv, wk)
                done += n

        wo_pref = [None]

        def pass2_chunk(s, j, stage):
            c0 = 8 + j * CH
            xkeys = ["xnT_t%d" % t for t in range(4 * j, 4 * j + 4)]
            hkeys = xkeys + ["xnT_hl", "xnT_hr"] + (["xnT_t%d" % (4 * j - 1)] if j > 0 else []) + (["xnT_t%d" % (4 * j + 4)] if j < 3 else [])
            SCL = 1.0 / math.sqrt(128.0)
            if stage == "P":
                def pv_evac(i, ps, pk, wv, wk):
                    cp("act", pvh[:, i, 0:CH], ps[:, :], reads=[pk], writes=["pvh%d" % i])
                    ph, phk = nextF()
                    for k in range(8):
                        mm(ph[:, 0:16], wv[:, k, (i % 2) * 128:(i % 2 + 1) * 128], xnT[:, k, c0 + CH - 8:c0 + CH + 8], k == 0, k == 7,
                           reads=hkeys + [wk], writes=[phk])
                    cp("act", pvh[:, i, CH:CH + 16], ph[:, 0:16], reads=[phk], writes=["pvh%d" % i])

                proj_fm(WIN_OFF["pv"], 8, c0 - 8, pv_evac, hkeys)
                proj_fm(WIN_OFF["pg"], 8, c0, lambda i, ps, pk, wv, wk: act(pgs[:, i, :], ps[:, :], AF.Silu, reads=[pk], writes=["pgs%d" % i]), xkeys)
                W = CH + 16
                for g, w in enumerate((2, 4, 8, 16)):
                    u = pvh[:, 2 * g:2 * g + 2, :]
                    uk = ["pvh%d" % (2 * g), "pvh%d" % (2 * g + 1)]
                    tt("dve", tA[:, :, 1:W], u[:, :, 1:W], u[:, :, 0:W - 1], ALU.add, reads=uk, writes=["tA"])
                    fin, fk = tA, "tA"
                    if w >= 4:
                        if w == 4:
                            tt("dve", tB[:, :, 8:8 + CH], tA[:, :, 9:9 + CH], tA[:, :, 7:7 + CH], ALU.add, reads=["tA"], writes=["tB"])
                            fin, fk = tB, "tB"
                        else:
                            tt("dve", tB[:, :, 3:W], tA[:, :, 3:W], tA[:, :, 1:W - 2], ALU.add, reads=["tA"], writes=["tB"])
                            if w == 8:
                                tt("dve", tC[:, :, 8:8 + CH], tB[:, :, 11:11 + CH], tB[:, :, 7:7 + CH], ALU.add, reads=["tB"], writes=["tC"])
                                fin, fk = tC, "tC"
                            else:
                                tt("dve", tC[:, :, 7:W], tB[:, :, 7:W], tB[:, :, 3:W - 4], ALU.add, reads=["tB"], writes=["tC"])
                                tt("dve", tA[:, :, 8:8 + CH], tC[:, :, 15:15 + CH], tC[:, :, 7:7 + CH], ALU.add, reads=["tC"], writes=["tA"])
                                fin, fk = tA, "tA"
                    pk_ = ["pT%d" % (2 * g), "pT%d" % (2 * g + 1)]
                    stt("dve", pT[:, 2 * g:2 * g + 2, :], fin[:, :, 8:8 + CH], 1.0 / w, u[:, :, 8:8 + CH], ALU.mult, ALU.subtract, reads=[fk] + uk, writes=pk_)
                    for edge, (m0, e0) in enumerate(((8, 0), (8 + CH - 8, 8))):
                        if (edge == 0 and j == 0) or (edge == 1 and j == 3):
                            ic = icn[:, s, g, e0:e0 + 8].unsqueeze(1).to_broadcast([128, 2, 8])
                            tt("dve", fin[:, :, m0:m0 + 8], fin[:, :, m0:m0 + 8], ic, ALU.mult, reads=[fk, "const"], writes=[fk])
                            tt("dve", pT[:, 2 * g:2 * g + 2, m0 - 8:m0], fin[:, :, m0:m0 + 8], u[:, :, m0:m0 + 8], ALU.subtract, reads=[fk] + uk, writes=pk_)
            if stage == "G":
                for g in range(4):
                    pk_ = ["pT%d" % (2 * g), "pT%d" % (2 * g + 1)]
                    for mo in range(2):
                        ps, pk = nextF()
                        for ki in range(2):
                            mm(ps[:, :], wpg[:, 2 * g + ki, mo * 128:(mo + 1) * 128], pT[:, 2 * g + ki, :], ki == 0, ki == 1, reads=pk_ + ["wpg"], writes=[pk])
                        ci = 2 * g + mo
                        stt("dve", gaT[:, ci, :], ps[:, :], NCOL(16 + ci), pgs[:, ci, :], ALU.mult, ALU.mult, reads=[pk, "const", "pgs%d" % ci], writes=["gaT"])
            if stage == "A":
                proj_fm(WIN_OFF["q"], 4, c0, lambda i, ps, pk, wv, wk: act(qT[:, i, :], ps[:, :], AF.Copy, reads=[pk], writes=["qT"]), xkeys)
                proj_fm(WIN_OFF["ag"], 4, c0, lambda i, ps, pk, wv, wk: act(ags[:, i, :], ps[:, :], AF.Silu, reads=[pk], writes=["ags"]), xkeys)
                def att_scores(h):
                    es = h % 2
                    ek = "E%d" % es
                    for mt in range(2):
                        ps, pk = nextF()
                        mm(ps[:, :], KT[:, h, mt * 128:(mt + 1) * 128], qT[:, h, :], True, True, reads=["KT", "qT"], writes=[pk])
                        act(Ee[es][:, mt, :], ps[:, :], AF.Exp, reads=[pk], writes=[ek + "_%d" % mt], scale=SCL)

                def att_pv(h):
                    es = h % 2
                    ek = "E%d" % es
                    po, pok = nextF()
                    for mt in range(2):
                        mm(po[:, :], Vv[:, mt, h * 128:(h + 1) * 128], Ee[es][:, mt, :], mt == 0, mt == 1, reads=["Vv", ek + "_%d" % mt], writes=[pok])
                    pd, pdk = nextF()
                    for mt in range(2):
                        mm(pd[:, :], ones[:, :], Ee[es][:, mt, :], mt == 0, mt == 1, reads=["ones", ek + "_%d" % mt], writes=[pdk])
                    S.op("dve", lambda e, pd=pd: e.reciprocal(out=rden[:, :], in_=pd[:, :]), reads=[pdk], writes=["rden"])
                    tt("dve", otmp[:, :], po[:, :], rden[:, :], ALU.mult, reads=[pok, "rden"], writes=["otmp"])
                    tt("pool", gcT[:, h, :], otmp[:, :], ags[:, h, :], ALU.mult, reads=["otmp", "ags"], writes=["gcT"])

                att_scores(0)
                for h in range(4):
                    if h + 1 < 4:
                        att_scores(h + 1)
                    att_pv(h)
                def fg_evac(i, ps, pk, wv, wk):
                    act(fgs[:, i, :], ps[:, :], AF.Silu, reads=[pk], writes=["fgs%d" % i])
                    tt("dve", gbT[:, i, :], GT[:, i, j * CH:(j + 1) * CH], fgs[:, i, :], ALU.mult, reads=["GT_%d" % j, "fgs%d" % i], writes=["gbT"])

                proj_fm(WIN_OFF["fg"], 4, c0, fg_evac, xkeys)
            if stage == "M":
                for mp in range(4):
                    wg = [load_w(w_cols(wb_in, 4096 + b * 1024 + mp * 256, 256)[0], (8, 256)) for b in range(3)]
                    wpo_, kpo = load_w(w_cols(wb_po, mp * 256, 256)[0], (8, 256))
                    i = rr["w"]
                    rr["w"] = (i + 1) % NSLOT
                    wfa = wslot[i][:, 0:2048].rearrange("p (k c) -> p k c", c=256)
                    kfa = "wslot%d" % i
                    dma("sp", wfa[:, 0:4, :], wb_fo[mp, :, :].rearrange("p (k c) -> p k c", k=4), reads=["wcast"], writes=[kfa], semkey=kfa)
                    dma("sp", wfa[:, 4:8, :], wb_ao[mp, :, :].rearrange("p (k c) -> p k c", k=4), reads=["wcast"], writes=[kfa], semkey=kfa)
                    for mm_ in range(2):
                        m = 2 * mp + mm_
                        cs = slice(mm_ * 128, (mm_ + 1) * 128)
                        gs_ = m % 2
                        for b in range(3):
                            wg_, wgk = wg[b]
                            ps, pk = nextF()
                            for k in range(8):
                                mm(ps[:, :], wg_[:, k, cs], xnT[:, k, c0:c0 + CH], k == 0, k == 7, reads=xkeys + [wgk], writes=[pk])
                            act(gate[gs_][:, b, :], ps[:, :], AF.Sigmoid, reads=[pk, "const"], writes=["gate%d_%d" % (gs_, b)], bias=NCOL(24 + b * 8 + m))
                        pa, pak = nextF()
                        for k in range(8):
                            mm(pa[:, :], wpo_[:, k, cs], gaT[:, k, :], k == 0, k == 7, reads=["gaT", kpo], writes=[pak])
                        tt("dve", m1[:, :], pa[:, :], gate[gs_][:, 0, :], ALU.mult, reads=[pak, "gate%d_0" % gs_], writes=["m1"])
                        pb, pbk = nextF()
                        for k in range(4):
                            mm(pb[:, :], wfa[:, k, cs], gbT[:, k, :], k == 0, k == 3, reads=["gbT", kfa], writes=[pbk])
                        tt("dve", m2[:, :], pb[:, :], gate[gs_][:, 1, :], ALU.mult, reads=[pbk, "gate%d_1" % gs_], writes=["m2"])
                        pc, pck = nextF()
                        for k in range(4):
                            mm(pc[:, :], wfa[:, 4 + k, cs], gcT[:, k, :], k == 0, k == 3, reads=["gcT", kfa], writes=[pck])
                        tt("dve", m3[:, :], pc[:, :], gate[gs_][:, 2, :], ALU.mult, reads=[pck, "gate%d_2" % gs_], writes=["m3"])
                        tt("pool", m1[:, :], m1[:, :], m2[:, :], ALU.add, reads=["m1", "m2"], writes=["m1"])
                        tt("pool", mT[:, m, :], m1[:, :], m3[:, :], ALU.add, reads=["m1", "m3"], writes=["mT"])
                wo_pref[0] = [load_w(w_cols(wb_o, q * 256, 256)[0], (8, 256)) for q in range(4)]
            if stage == "O":
                wo = wo_pref[0]
                info = {}
                if j + 1 < 4:
                    for col_ in (WIN_OFF["q"], WIN_OFF["q"] + 256, WIN_OFF["ag"], WIN_OFF["ag"] + 256):
                        wpref[col_] = load_w(w_cols(wb_in, col_, 256)[0], (8, 256))

                def o_main(tl):
                    t = 4 * j + tl
                    sl = next_x()
                    xk = "xin%d" % sl
                    dma("sp", xin[sl][:, :], xs[s, t * 128:(t + 1) * 128, :], reads=[], writes=[xk], semkey=xk)
                    hs = t % 2
                    hk = "hout%d" % hs
                    S.op("pool", lambda e, hs=hs: e.memset(stat[:, 4 + hs, 0:2], 0.0), writes=["sf%dssa" % hs])
                    for hf in range(2):
                        ps, pk = nextF()
                        for qq in range(2):
                            wv, wk = wo[2 * hf + qq]
                            for k in range(8):
                                mm(ps[:, qq * 256:(qq + 1) * 256], mT[:, k, tl * 128:(tl + 1) * 128], wv[:, k, :], k == 0, k == 7, reads=["mT", wk], writes=[pk])
                        tt("dve", hout[hs][:, hf * 512:(hf + 1) * 512], ps[:, :], xin[sl][:, hf * 512:(hf + 1) * 512], ALU.add, reads=[pk, xk], writes=[hk])
                        act(xin[sl][:, hf * 512:(hf + 1) * 512], hout[hs][:, hf * 512:(hf + 1) * 512], AF.Square, reads=[hk], writes=[xk, "sf%dssa" % hs],
                            accum_out=stat[:, 4 + hs, hf:hf + 1])
                    tt("dve", stat[:, 4 + hs, 2:3], stat[:, 4 + hs, 0:1], stat[:, 4 + hs, 1:2], ALU.add, reads=["sf%dssa" % hs], writes=["sf%dss" % hs])
                    ts("dve", stat[:, 4 + hs, 3:4], stat[:, 4 + hs, 2:3], 1.0 / D, EPS, ALU.mult, ALU.add, reads=["sf%dss" % hs], writes=["sf%dms" % hs])
                    tt("pool", stat[:, 4 + hs, 4:5], stat[:, 4 + hs, 3:4], neghalf[:, :], ALU.pow, reads=["sf%dms" % hs, "neghalf"], writes=["sf%dy" % hs])

                def o_fin(tl):
                    t = 4 * j + tl
                    hs = t % 2
                    hk = "hout%d" % hs
                    key = "sf%d" % hs
                    ms_ap, y_ap, t_ap = stat[:, 4 + hs, 3:4], stat[:, 4 + hs, 4:5], stat[:, 4 + hs, 5:6]
                    for _ in range(2):
                        tt("dve", t_ap, y_ap, y_ap, ALU.mult, reads=[key + "y"], writes=[key + "t"])
                        tt("dve", t_ap, t_ap, ms_ap, ALU.mult, reads=[key + "t", key + "ms"], writes=[key + "t"])
                        ts("dve", t_ap, t_ap, -0.5, 1.5, ALU.mult, ALU.add, reads=[key + "t"], writes=[key + "t"])
                        tt("dve", y_ap, y_ap, t_ap, ALU.mult, reads=[key + "y", key + "t"], writes=[key + "y"])
                    stt("dve", hout[hs][:, :], hout[hs][:, :], y_ap, nfr[:, :], ALU.mult, ALU.mult, reads=[hk, key + "y", "const"], writes=[hk])
                    dma("act", yout[s, t * 128:(t + 1) * 128, :], hout[hs][:, :], reads=[hk], writes=[], semkey="yst%d" % hs)

                for tl in range(5):
                    if tl < 4:
                        o_main(tl)
                    if tl >= 1:
                        o_fin(tl - 1)

        def fv_to_scratch(s):
            def f(t, tile, key):
                dma("act", fvS[s, t * 128:(t + 1) * 128, :], tile[:, :], reads=[key], writes=["fvS%d_%d" % (s, t)], semkey="fvst%d" % (t % 2))
            return f

        def pass2_segment(s):
            rr["nF"] = 8
            S.rgn_keys = set(["pvh%d" % i for i in range(8)] + ["tA", "tB", "tC", "qT", "ags", "rden", "otmp"]
                             + ["E%d_%d" % (a_, b_) for a_ in range(2) for b_ in range(2)] + ["fgs%d" % i for i in range(4)])
            marker = lambda: S.op("pool", lambda e: e.memset(stat[:, 7, 0:1], 0.0), writes=["RGN"])
            marker()
            pass2_chunk(s, 0, "P")
            pass2_chunk(s, 0, "G")
            for j in range(4):
                marker()
                pass2_chunk(s, j, "A")
                marker()
                if j + 1 < 4:
                    pass2_chunk(s, j + 1, "P")
                pass2_chunk(s, j, "M")
                if j + 1 < 4:
                    pass2_chunk(s, j + 1, "G")
                pass2_chunk(s, j, "O")
            S.rgn_keys = set()

        sample_fv_all()
        S.fence()
        fourier_sample_local()
        S.fence()
        pass1(2, None, aux_only=True)
        S.fence()
        pass2_segment(2)
        S.fence()
        for s_ in range(2):
            pass1(s_, fv_to_scratch(s_))
            S.fence()
            fourier_prompt(s_)
            S.fence()
            pass2_segment(s_)
            S.fence()

        S.finalize()
        S.emit({"act": ["yst0", "yst1"]})
    return nc


def _tables(r=0):
    bf = ml_dtypes.bfloat16
    p = np.arange(128, dtype=np.float64)
    ang = 2 * np.pi * np.outer(p, p) / 128.0
    C, Sn = np.cos(ang), np.sin(ang)
    dfta = np.concatenate([C, -Sn], axis=1).astype(np.float32).astype(bf)
    k = np.arange(128, dtype=np.float64)
    tw = np.zeros((128, 2, 2, 2, 128), np.float64)
    i_p = (np.arange(128) // 8).astype(np.float64)
    rot = 2 * np.pi * np.outer(np.ones(128), k) * (r + 1) / 8.0
    for v_, a_ in enumerate((2 * np.pi * np.outer(i_p, k) / 2048.0, 2 * np.pi * np.outer(p, k) / 16384.0 + rot)):
        tw[:, v_, 0, 0], tw[:, v_, 0, 1] = np.cos(a_), np.sin(a_)
        tw[:, v_, 1, 0], tw[:, v_, 1, 1] = -np.sin(a_), np.cos(a_)
    kbm = np.zeros((128, 2, 2, 256), np.float64)
    i16 = np.arange(16, dtype=np.float64)
    a = 2 * np.pi * np.outer(i16, i16) / 16.0
    kc = np.kron(np.cos(a), np.eye(8))
    ks = np.kron(-np.sin(a), np.eye(8))
    kbm[:, 0, 0] = np.concatenate([kc, ks], axis=1)
    kbm[:, 0, 1] = np.concatenate([-ks, kc], axis=1)
    kbm[:, 1, 0] = np.concatenate([C, -Sn], axis=1)
    kbm[:, 1, 1] = np.concatenate([Sn, C], axis=1)
    cs = np.concatenate([C, Sn], axis=1).astype(np.float32)
    return {
        "t_ident": np.eye(128, dtype=np.float32).astype(bf),
        "t_dfta": dfta,
        "t_twid": tw.reshape(128, -1).astype(np.float32),
        "t_kb": kbm.reshape(128, -1).astype(np.float32).astype(bf),
        "t_cs": cs,
    }


def _kbo(r):
    i = np.arange(128, dtype=np.float64)
    kh = np.arange(16 * r, 16 * r + 16, dtype=np.float64)
    a = 2 * np.pi * np.outer(i, kh) / 128.0
    kc, ks = np.cos(a), -np.sin(a)
    m = np.concatenate([kc, ks, -ks, kc], axis=1)
    return m.astype(np.float32).astype(ml_dtypes.bfloat16)


def _invcnt():
    out = np.zeros((NCORES, 3, 4, 16), np.float32)
    for r in range(NCORES):
        for s in range(3):
            if s < 2:
                S_, base = SEG, 0
            else:
                S_, base = 16384, SEG * r
            for g, w in enumerate((2, 4, 8, 16)):
                half = w // 2
                for e in range(16):
                    t = base + (e if e < 8 else SEG - 16 + e)
                    lo = min(max(t - half, 0), S_)
                    hi = min(max(t + half, 0), S_)
                    out[r, s, g, e] = 1.0 / float(hi - lo)
    return out


_NC_CACHE = {}


def kernel(x_prompt, x_sample, mem_prompt, mem_sample, norm_in, norm_mem, w_in, w_pool_grp,
           pool_scale, w_four_grp, w_kv, w_pool_out, w_four_out, w_attn_out, b_gate, w_o, norm_f):
    f = lambda a: np.ascontiguousarray(np.asarray(a), dtype=np.float32)
    x_prompt, x_sample, mem_prompt, mem_sample = f(x_prompt), f(x_sample), f(mem_prompt), f(mem_sample)
    tabs = _tables()
    tabs.pop("t_twid")
    icnt = _invcnt()
    cols = np.concatenate([
        f(norm_in)[0].reshape(8, 128).T, f(norm_mem)[0].reshape(8, 128).T,
        f(pool_scale)[0].reshape(8, 128).T, f(b_gate)[0].reshape(24, 128).T], axis=1)
    shared = {
        "w_in": f(w_in)[0], "w_kv": f(w_kv)[0], "w_po": f(w_pool_out)[0], "w_fo": f(w_four_out)[0],
        "w_ao": f(w_attn_out)[0], "w_o": f(w_o)[0], "w_pg": f(w_pool_grp)[0].reshape(1024, 256),
        "w_fg": np.ascontiguousarray(f(w_four_grp)[0].transpose(1, 0, 2)),
        "cols": np.ascontiguousarray(cols), "nf_row": f(norm_f).reshape(1, D),
    }
    shared.update(tabs)
    in_maps = []
    zeros8 = np.zeros((8, D), np.float32)
    for r in range(NCORES):
        xs = np.stack([x_prompt[2 * r], x_prompt[2 * r + 1], x_sample[0, SEG * r:SEG * (r + 1)]])
        xh = np.zeros((3, 16, D), np.float32)
        xh[2, 0:8] = x_sample[0, SEG * r - 8:SEG * r] if r > 0 else zeros8
        xh[2, 8:16] = x_sample[0, SEG * (r + 1):SEG * (r + 1) + 8] if r < NCORES - 1 else zeros8
        mem = np.stack([mem_prompt[2 * r], mem_prompt[2 * r + 1], mem_sample[0]])
        m = {"xs": xs, "xh": xh, "mem": mem, "icnt": icnt[r].reshape(1, -1), "t_kbo": _kbo(r),
             "xfull": np.ascontiguousarray(np.roll(x_sample[0], -SEG * (r + 1), axis=0)), "t_twid": _tables(r)["t_twid"]}
        m.update(shared)
        in_maps.append(m)
    if "nc" not in _NC_CACHE:
        _NC_CACHE["nc"] = build_program()
    res = run_bass_kernel_spmd(_NC_CACHE["nc"], in_maps, core_ids=list(range(NCORES)))
    y_prompt = np.empty((16, SEG, D), np.float32)
    y_sample = np.empty((1, 16384, D), np.float32)
    for r in range(NCORES):
        yo = res.results[r]["yout"]
        y_prompt[2 * r] = yo[0]
        y_prompt[2 * r + 1] = yo[1]
        y_sample[0, SEG * r:SEG * (r + 1)] = yo[2]
    return (y_prompt, y_sample)
```

```python
import math
from contextlib import ExitStack

import numpy as np
import ml_dtypes

import concourse.bass as bass
import concourse.mybir as mybir
from concourse.bass_utils import run_bass_kernel_spmd

F32 = mybir.dt.float32
BF16 = mybir.dt.bfloat16
AF = mybir.ActivationFunctionType
ALU = mybir.AluOpType

NCORES = 8
D = 1024
SEG = 2048
NT = 16
CH = 512
EPS = 1e-6
ENGS = ("pe", "act", "dve", "pool", "sp")


class _Op:
    __slots__ = ("eng", "fn", "deps", "dma", "semkey", "need_inc", "val", "waits", "grp_total", "inc")

    def __init__(self, eng, fn, deps, dma, semkey, grp_total, inc=16):
        self.eng, self.fn, self.deps, self.dma, self.semkey = eng, fn, deps, dma, semkey
        self.inc = inc
        self.need_inc = False
        self.val = None
        self.waits = []
        self.grp_total = grp_total


class Sched:
    def __init__(self, nc):
        self.nc = nc
        self.ops = []
        self.last_w = {}
        self.reads = {}
        self.fence_ops = set()
        self.last_eng = {}
        self.open_dma = []
        self.rgn_keys = set()

    def op(self, eng, fn, reads=(), writes=(), dma=False, semkey=None, grp_total=False, inc=16, nofence=False):
        if self.rgn_keys and (self.rgn_keys & (set(reads) | set(writes))) and "RGN" not in writes:
            reads = list(reads) + ["RGN"]
        deps = set() if nofence else set(self.fence_ops)
        for k in reads:
            w = self.last_w.get(k)
            if w is not None:
                deps.add(w)
        for k in writes:
            w = self.last_w.get(k)
            if w is not None:
                deps.add(w)
            deps.update(self.reads.get(k, ()))
        o = _Op(eng, fn, deps, dma, semkey, grp_total, inc)
        self.ops.append(o)
        for k in reads:
            self.reads.setdefault(k, []).append(o)
        for k in writes:
            self.last_w[k] = o
            self.reads[k] = []
        if dma:
            self.open_dma.append(o)
        else:
            self.last_eng[eng] = o
        return o

    def fence(self):
        self.fence_ops = set(self.last_eng.values()) | set(self.open_dma)
        self.open_dma = []

    def finalize(self):
        for o in self.ops:
            o.deps = {d for d in o.deps if not (o.eng == "pe" and d.eng == "pe" and not d.dma and not o.dma)}
            for d in o.deps:
                if not d.dma:
                    d.need_inc = True
        eng_cnt = {e: 0 for e in ENGS}
        dma_cnt = {}
        for o in self.ops:
            if o.dma:
                dma_cnt[o.semkey] = dma_cnt.get(o.semkey, 0) + o.inc
                o.val = dma_cnt[o.semkey]
            elif o.need_inc:
                eng_cnt[o.eng] += 1
                o.val = eng_cnt[o.eng]
        for o in self.ops:
            if o.dma and o.grp_total:
                o.val = dma_cnt[o.semkey]
        known = {e: {} for e in ENGS}
        for o in self.ops:
            need = {}
            for d in o.deps:
                key = ("dma", d.semkey) if d.dma else ("eng", d.eng)
                if need.get(key, 0) < d.val:
                    need[key] = d.val
            kn = known[o.eng]
            for key, v in need.items():
                if kn.get(key, 0) >= v:
                    continue
                kn[key] = v
                o.waits.append((key, v))
        self.dma_keys = sorted(dma_cnt.keys(), key=str)
        self.dma_cnt = dma_cnt

    def emit(self, final_keys):
        nc = self.nc
        with ExitStack() as st:
            esem = {e: st.enter_context(nc.semaphore("s_" + e)) for e in ENGS}
            dsem = {k: st.enter_context(nc.semaphore("d%d" % i)) for i, k in enumerate(self.dma_keys)}
            block = st.enter_context(nc.Block())
            by_eng = {e: [o for o in self.ops if o.eng == e] for e in ENGS}

            def run(e, eng):
                for o in by_eng[e]:
                    for key, v in o.waits:
                        eng.wait_ge(dsem[key[1]] if key[0] == "dma" else esem[key[1]], v)
                    inst = o.fn(eng)
                    if o.dma:
                        inst.then_inc(dsem[o.semkey], o.inc)
                    elif o.need_inc:
                        inst.then_inc(esem[e], 1)
                for k in final_keys.get(e, ()):
                    eng.wait_ge(dsem[k], self.dma_cnt[k])

            block.tensor(lambda eng: run("pe", eng))
            block.scalar(lambda eng: run("act", eng))
            block.vector(lambda eng: run("dve", eng))
            block.gpsimd(lambda eng: run("pool", eng))
            block.sync(lambda eng: run("sp", eng))


def build_program():
    nc = bass.Bass("TRN2", target_bir_lowering=False)
    S = Sched(nc)

    def din(name, shape, dt=F32):
        return nc.dram_tensor(name, list(shape), dt, kind="ExternalInput")

    xs = din("xs", [3, SEG, D])
    xh = din("xh", [3, 16, D])
    mem = din("mem", [3, 256, D])
    xfull = din("xfull", [16384, D])
    t_kbo = din("t_kbo", [128, 64], BF16)
    w_in = din("w_in", [D, 7168])
    w_kv = din("w_kv", [D, 1024])
    w_po = din("w_po", [D, 1024])
    w_fo = din("w_fo", [512, 1024])
    w_ao = din("w_ao", [512, 1024])
    w_o = din("w_o", [D, 1024])
    w_pg = din("w_pg", [1024, 256])
    w_fg = din("w_fg", [128, 4, 128])
    cols = din("cols", [128, 48])
    nf_row = din("nf_row", [1, D])
    icnt = din("icnt", [1, 3 * 4 * 16])
    t_ident = din("t_ident", [128, 128], BF16)
    t_dfta = din("t_dfta", [128, 256], BF16)
    t_twid = din("t_twid", [128, 2 * 2 * 256])
    t_kb = din("t_kb", [128, 2 * 2 * 256], BF16)
    t_cs = din("t_cs", [128, 256])
    yout = nc.dram_tensor("yout", [3, SEG, D], F32, kind="ExternalOutput")

    wb_in = nc.dram_tensor("wb_in", [28, 128, 2048], BF16)
    wb_kv = nc.dram_tensor("wb_kv", [4, 128, 2048], BF16)
    wb_po = nc.dram_tensor("wb_po", [4, 128, 2048], BF16)
    wb_fo = nc.dram_tensor("wb_fo", [4, 128, 1024], BF16)
    wb_ao = nc.dram_tensor("wb_ao", [4, 128, 1024], BF16)
    wb_o = nc.dram_tensor("wb_o", [4, 128, 2048], BF16)
    wb_pg = nc.dram_tensor("wb_pg", [1024, 256], BF16)
    fvS = nc.dram_tensor("fvS", [2, SEG, 512], BF16)
    G1 = nc.dram_tensor("G1", [16384, 512], BF16)

    st = ExitStack()
    with st:
        def sb(name, shape, dt):
            return st.enter_context(nc.sbuf_tensor(name, list(shape), dt))

        ident = sb("ident", [128, 128], BF16)
        ones = sb("ones", [128, 128], BF16)
        dfta = sb("dfta", [128, 256], BF16)
        twid2 = sb("twid", [128, 1024], F32)
        twid = twid2[:, :].rearrange("p (a b c d) -> p a b c d", a=2, b=2, c=2)
        kb2 = sb("kb", [128, 1024], BF16)
        kb = kb2[:, :].rearrange("p (a b c) -> p a b c", a=2, b=2)
        kbo2 = sb("kbo", [128, 64], BF16)
        kbo = kbo2[:, :].rearrange("p (a b) -> p a b", a=2)
        csm2 = sb("csm", [128, 256], F32)
        csm = csm2[:, :].rearrange("p (a b) -> p a b", a=2)
        wfg2 = sb("wfg", [128, 512], F32)
        wfg = wfg2[:, :].rearrange("p (a b) -> p a b", a=4)
        fold = sb("fold", [128, 2, 4, 2, 128], BF16)
        colsb = sb("colsb", [128, 48], F32)
        nfr = sb("nfr", [128, D], F32)
        icn2 = sb("icn", [128, 192], F32)
        icn = icn2[:, :].rearrange("p (a b c) -> p a b c", a=3, b=4)
        wpg = sb("wpg", [128, 8, 256], BF16)
        neghalf = sb("neghalf", [128, 1], F32)
        pvsave = sb("pvsave", [128, 8, 16], BF16)
        xnT = sb("xnT", [128, 8, SEG + 16], BF16)
        GT = sb("GT", [128, 4, SEG], BF16)
        KT = sb("KT", [128, 4, 256], BF16)
        Vv = sb("Vv", [128, 2, 512], BF16)
        mnT = sb("mnT", [128, 8, 256], BF16)
        NSLOT = 9
        wslot = [sb("wslot%d" % i, [128, 2048], BF16) for i in range(NSLOT)]
        xin = [sb("xin%d" % i, [128, D], F32) for i in range(2)]
        hout = [sb("hout%d" % i, [128, D], F32) for i in range(2)]
        stat = sb("stat", [128, 8, 24], F32)
        ARENA = 37 * 1024
        arena = sb("arena", [128, ARENA], BF16)
        apos = [0]

        def carve(shape, dt):
            n = int(np.prod(shape))
            nb = n * (2 if dt == F32 else 1)
            a = arena[:, apos[0]:apos[0] + nb]
            apos[0] += nb
            assert apos[0] <= ARENA, apos[0]
            if dt == F32:
                a = a.bitcast(F32)
            if len(shape) == 2:
                return a.rearrange("p (a b) -> p a b", b=shape[1])
            if len(shape) == 3:
                return a.rearrange("p (a b c) -> p a b c", b=shape[1], c=shape[2])
            return a

        apos[0] = 0
        Dt = carve([16 * 512], BF16)
        Fsb = carve([2 * 16 * 512], BF16)
        Yt = [carve([2, 128], BF16) for _ in range(3)]
        tw4 = [carve([4, 128], F32) for _ in range(3)]
        posX = apos[0]
        Dt2 = carve([16 * 512], BF16)
        apos[0] = posX
        xsb = [carve([D], BF16) for _ in range(2)]
        fvt = [carve([512], BF16) for _ in range(2)]
        FTt = carve([8, 512], BF16)
        endA = apos[0]
        apos[0] = 16 * 512
        xinP = list(xin) + [carve([D], F32) for _ in range(4)]
        xsbP = list(xsb) + [carve([D], BF16) for _ in range(2)]
        sqjunk = carve([D], BF16)
        apos[0] = 0
        gaT = carve([8, CH], BF16)
        gcT = carve([4, CH], BF16)
        gbT = carve([4, CH], BF16)
        gate = [carve([3, CH], BF16) for _ in range(2)]
        m1 = carve([CH], F32)
        m2 = carve([CH], F32)
        m3 = carve([CH], F32)
        mT = carve([8, CH], BF16)
        posR = apos[0]
        pvh = carve([8, CH + 16], BF16)
        tA = carve([2, CH + 16], F32)
        tB = carve([2, CH + 16], F32)
        tC = carve([2, CH + 16], F32)
        pT = carve([8, CH], BF16)
        pgs = carve([8, CH], BF16)
        endB1 = apos[0]
        apos[0] = posR
        qT = carve([4, CH], BF16)
        ags = carve([4, CH], BF16)
        Ee = [carve([2, CH], BF16) for _ in range(2)]
        rden = carve([CH], F32)
        otmp = carve([CH], F32)
        fgs = carve([4, CH], BF16)
        endB = max(endB1, apos[0])

        psW = [st.enter_context(nc.psum_tensor("psW%d" % i, [128, 1024], F32)) for i in range(4)]
        psF = [psW[i // 2][:, (i % 2) * 512:(i % 2 + 1) * 512] for i in range(8)]
        psT = [psW[2], psW[3]]
        rr = {"F": 0, "T": 0, "w": 0, "nF": 8}

        def nextF():
            i = rr["F"] % rr["nF"]
            rr["F"] = (i + 1) % rr["nF"]
            return psF[i], "psF%d" % i

        class _Wide:
            def __init__(self, t, keys):
                self.t, self.keys = t, keys

        def nextT():
            i = rr["T"]
            rr["T"] = 1 - i
            return psT[i], ["psF%d" % (4 + 2 * i), "psF%d" % (5 + 2 * i)]

        _pid = {}

        def getpid(e):
            k = id(e)
            if k not in _pid:
                _pid[k] = e.partition_id()
            return _pid[k]

        def dma(eng, out, in_, reads, writes, semkey, grp_total=False, nofence=False):
            return S.op(eng, lambda e, out=out, in_=in_: e.dma_start(out=out, in_=in_),
                        reads=reads, writes=writes, dma=True, semkey=semkey, grp_total=grp_total, nofence=nofence)

        def mm(out, lhsT, rhs, start, stop, reads, writes):
            return S.op("pe", lambda e, out=out, lhsT=lhsT, rhs=rhs, start=start, stop=stop:
                        e.matmul(out, lhsT=lhsT, rhs=rhs, start=start, stop=stop), reads=reads, writes=writes)

        def act(out, in_, func, reads, writes, bias=None, scale=None, accum_out=None):
            kw = {}
            if bias is not None:
                kw["bias"] = bias
            if scale is not None:
                kw["scale"] = scale
            if accum_out is not None:
                kw["accum_out"] = accum_out
            return S.op("act", lambda e, out=out, in_=in_, func=func, kw=kw: e.activation(out=out, in_=in_, func=func, **kw),
                        reads=reads, writes=writes)

        def tt(eng, out, in0, in1, op, reads, writes):
            return S.op(eng, lambda e, out=out, in0=in0, in1=in1, op=op: e.tensor_tensor(out=out, in0=in0, in1=in1, op=op),
                        reads=reads, writes=writes)

        def ts(eng, out, in0, s1, s2, op0, op1, reads, writes):
            if s2 is None:
                return S.op(eng, lambda e, out=out, in0=in0, s1=s1, op0=op0: e.tensor_scalar(out=out, in0=in0, scalar1=s1, scalar2=None, op0=op0),
                            reads=reads, writes=writes)
            return S.op(eng, lambda e, out=out, in0=in0, s1=s1, s2=s2, op0=op0, op1=op1:
                        e.tensor_scalar(out=out, in0=in0, scalar1=s1, scalar2=s2, op0=op0, op1=op1), reads=reads, writes=writes)

        def stt(eng, out, in0, scalar, in1, op0, op1, reads, writes):
            return S.op(eng, lambda e, out=out, in0=in0, scalar=scalar, in1=in1, op0=op0, op1=op1:
                        e.scalar_tensor_tensor(out=out, in0=in0, scalar=scalar, in1=in1, op0=op0, op1=op1), reads=reads, writes=writes)

        def cp(eng, out, in_, reads, writes):
            if eng == "act":
                return act(out, in_, AF.Copy, reads, writes)
            return S.op(eng, lambda e, out=out, in_=in_: e.tensor_copy(out=out, in_=in_), reads=reads, writes=writes)

        def load_w(dram_ap, shape3, dep="wcast"):
            i = rr["w"]
            rr["w"] = (i + 1) % NSLOT
            n = shape3[0] * shape3[1]
            v = wslot[i][:, 0:n].rearrange("p (k c) -> p k c", c=shape3[1])
            key = "wslot%d" % i
            dma("sp", v, dram_ap, reads=[dep], writes=[key], semkey=key, nofence=True)
            return v, key

        def load_l(li, dram_ap, shape3):
            n = shape3[0] * shape3[1]
            v = lslot[li][:, 0:n].rearrange("p (k c) -> p k c", c=shape3[1])
            key = "lslot%d" % li
            dma("sp", v, dram_ap, reads=["wcast"], writes=[key], semkey=key)
            return v, key

        def w_cols(wb, c0, ncol, kchunks=8):
            assert ncol == 256 and c0 % 256 == 0, (c0, ncol)
            return wb[c0 // 256, :, :].rearrange("p (k c) -> p k c", k=kchunks), (kchunks, ncol)

        def cast_blk(dst, src, cb, kk):
            return (dst[cb, :, :].rearrange("p (k c) -> p k c", k=kk), src[:, cb * 256:(cb + 1) * 256].rearrange("(k p) c -> p k c", p=128))

        for ci, cb in enumerate((8, 9)):
            o_, i_ = cast_blk(wb_in, w_in, cb, 8)
            dma("pool", o_, i_, reads=[], writes=(["wcast_fv"] if ci == 1 else []), semkey="wcast_fv", grp_total=True)
        casts = [cast_blk(wb_in, w_in, cb, 8) for cb in range(28) if cb not in (8, 9)]
        casts += [cast_blk(wb_kv, w_kv, cb, 8) for cb in range(4)]
        casts += [cast_blk(wb_po, w_po, cb, 8) for cb in range(4)]
        casts += [cast_blk(wb_fo, w_fo, cb, 4) for cb in range(4)]
        casts += [cast_blk(wb_ao, w_ao, cb, 4) for cb in range(4)]
        casts += [cast_blk(wb_o, w_o, cb, 8) for cb in range(4)]
        casts += [(wb_pg[:, :], w_pg[:, :])]
        cast_state = {"i": 0}

        def issue_casts(n):
            for _ in range(n):
                ci = cast_state["i"]
                if ci >= len(casts):
                    return
                o_, i_ = casts[ci]
                dma("pool", o_, i_, reads=[], writes=(["wcast"] if ci == len(casts) - 1 else []), semkey="wcast", grp_total=True)
                cast_state["i"] = ci + 1
        cl = lambda out, in_, last=False: dma("sp", out, in_, reads=[], writes=(["const"] if last else []), semkey="const", grp_total=True)
        cl(ident[:, :], t_ident[:, :])
        cl(dfta[:, :], t_dfta[:, :])
        cl(twid2[:, :], t_twid[:, :])
        cl(kb2[:, :], t_kb[:, :])
        cl(kbo2[:, :], t_kbo[:, :])
        cl(csm2[:, :], t_cs[:, :])
        cl(wfg2[:, :], w_fg[:, :, :].rearrange("p a b -> p (a b)"))
        cl(colsb[:, :], cols[:, :])
        cl(nfr[:, :], nf_row[0:1, :].partition_broadcast(128))
        cl(icn2[:, :], icnt[0:1, :].partition_broadcast(128), last=True)
        S.op("pool", lambda e: e.memset(ones[:, :], 1.0), writes=["ones"])
        S.op("pool", lambda e: e.memset(neghalf[:, :], -0.5), writes=["neghalf"])
        for h in range(4):
            for ab in range(2):
                ps, pk = nextF()
                mm(ps[:, 0:128], csm[:, ab, :], wfg[:, h, :], True, True, reads=["const"], writes=[pk])
                for v, Sq in enumerate((2048, 16384)):
                    act(fold[:, v, h, ab, :], ps[:, 0:128], AF.Copy, reads=[pk], writes=["fold"], scale=1.0 / math.sqrt(Sq * 128.0))

        NCOL = lambda k: colsb[:, k:k + 1]

        def rstd_from_ss(ss_ap, ms_ap, y_ap, t_ap, key):
            ts("dve", ms_ap, ss_ap, 1.0 / D, EPS, ALU.mult, ALU.add, reads=[key + "ss"], writes=[key + "ms"])
            tt("pool", y_ap, ms_ap, neghalf[0:ms_ap.shape[0], :], ALU.pow,
               reads=[key + "ms", "neghalf"], writes=[key + "y"])
            for _ in range(2):
                tt("dve", t_ap, y_ap, y_ap, ALU.mult, reads=[key + "y"], writes=[key + "t"])
                tt("dve", t_ap, t_ap, ms_ap, ALU.mult, reads=[key + "t", key + "ms"], writes=[key + "t"])
                ts("dve", t_ap, t_ap, -0.5, 1.5, ALU.mult, ALU.add, reads=[key + "t"], writes=[key + "t"])
                tt("dve", y_ap, y_ap, t_ap, ALU.mult, reads=[key + "y", key + "t"], writes=[key + "y"])

        def tile_pipeline(items, extra=None):
            rr["nF"] = 4
            rr["F"] = 0
            n = len(items)

            def xk_(i):
                return "xinP%d" % (i % 6)

            def sa(i):
                it = items[i]
                sl, npart = i % 4, it["npart"]
                xt = xinP[i % 6]
                dma("sp", xt[0:npart, :], it["src"], reads=[], writes=[xk_(i)], semkey=xk_(i))
                S.op("pool", lambda e, sl=sl: e.memset(stat[:, sl, 0:1], 0.0), writes=["st%dss" % sl])
                act(sqjunk[0:npart, :], xt[0:npart, :], AF.Square, reads=[xk_(i)], writes=["sqjunk", "st%dss" % sl],
                    accum_out=stat[0:npart, sl, 0:1])

            def sb_(i):
                it = items[i]
                sl, npart = i % 4, it["npart"]
                ts("dve", stat[0:npart, sl, 1:2], stat[0:npart, sl, 0:1], 1.0 / D, EPS, ALU.mult, ALU.add, reads=["st%dss" % sl], writes=["st%dms" % sl])
                tt("pool", stat[0:npart, sl, 2:3], stat[0:npart, sl, 1:2], neghalf[0:npart, :], ALU.pow,
                   reads=["st%dms" % sl, "neghalf"], writes=["st%dy" % sl])

            def sc(i):
                it = items[i]
                sl, npart = i % 4, it["npart"]
                key = "st%d" % sl
                ms_ap, y_ap, t_ap = stat[0:npart, sl, 1:2], stat[0:npart, sl, 2:3], stat[0:npart, sl, 3:4]
                for _ in range(2):
                    tt("dve", t_ap, y_ap, y_ap, ALU.mult, reads=[key + "y"], writes=[key + "t"])
                    tt("dve", t_ap, t_ap, ms_ap, ALU.mult, reads=[key + "t", key + "ms"], writes=[key + "t"])
                    ts("dve", t_ap, t_ap, -0.5, 1.5, ALU.mult, ALU.add, reads=[key + "t"], writes=[key + "t"])
                    tt("dve", y_ap, y_ap, t_ap, ALU.mult, reads=[key + "y", key + "t"], writes=[key + "y"])

            def sd(i):
                it = items[i]
                sl, npart = i % 4, it["npart"]
                act(xsbP[sl][0:npart, :], xinP[i % 6][0:npart, :], AF.Copy, reads=[xk_(i), "st%dy" % sl], writes=["xsbP%d" % sl],
                    scale=stat[0:npart, sl, 2:3])

            def se(i):
                it = items[i]
                sl, npart = i % 4, it["npart"]
                pt, pk = nextT()
                for k in range(8):
                    S.op("pe", lambda e, pt=pt, k=k, sl=sl, npart=npart:
                         e.matmul(pt[:, k * 128:k * 128 + npart], lhsT=xsbP[sl][0:npart, k * 128:(k + 1) * 128], rhs=ident[0:npart, 0:npart], start=True, stop=True),
                         reads=["xsbP%d" % sl, "const"], writes=pk)
                it["dst"](pt, pk)

            def sf(i):
                if items[i].get("post") is not None:
                    items[i]["post"]()

            stages = [sa, sb_, sc, sd, se, sf]
            for i in range(n + len(stages) - 1):
                for d_, fn_ in enumerate(stages):
                    if 0 <= i - d_ < n:
                        fn_(i - d_)
                if extra is not None:
                    extra(i)

        def xn_dst_t(t):
            return lambda pt, pk: xn_dst(pt, pk, t)

        def xn_dst(pt, pk, t):
            src = pt[:, :].rearrange("p (k c) -> p k c", c=128)
            tt("dve", xnT[:, :, 8 + t * 128: 8 + (t + 1) * 128], src,
               colsb[:, 0:8].unsqueeze(2).to_broadcast([128, 8, 128]), ALU.mult,
               reads=pk + ["const"], writes=["xnT_t%d" % t])

        def xn_halo_dst(pt, pk, _=None):
            src = pt[:, :].rearrange("p (k c) -> p k c", c=128)
            nb = colsb[:, 0:8].unsqueeze(2).to_broadcast([128, 8, 8])
            tt("dve", xnT[:, :, 0:8], src[:, :, 0:8], nb, ALU.mult, reads=pk + ["const"], writes=["xnT_hl"])
            tt("dve", xnT[:, :, SEG + 8:SEG + 16], src[:, :, 8:16], nb, ALU.mult, reads=pk + ["const"], writes=["xnT_hr"])

        def mn_dst_fn(mt):
            def f(pt, pk, _=None):
                src = pt[:, :].rearrange("p (k c) -> p k c", c=128)
                tt("dve", mnT[:, :, mt * 128:(mt + 1) * 128], src,
                   colsb[:, 8:16].unsqueeze(2).to_broadcast([128, 8, 128]), ALU.mult, reads=pk + ["const"], writes=["mnT"])
            return f

        xrr = [0]

        def next_x():
            i = xrr[0]
            xrr[0] = 1 - i
            return i

        def fv_post(t, tq, wv, wk, store_fn):
            def f():
                ps, pk = nextF()
                for q in range(2):
                    for k in range(8):
                        mm(ps[:, q * 256:(q + 1) * 256], xnT[:, k, 8 + tq * 128:8 + (tq + 1) * 128], wv[q][:, k, :], k == 0, k == 7,
                           reads=["xnT_t%d" % tq, wk[q]], writes=[pk])
                fs = t % 2
                act(fvt[fs][:, :], ps[:, :], AF.Copy, reads=[pk], writes=["fvt%d" % fs])
                store_fn(t, fvt[fs], "fvt%d" % fs)
            return f

        def pass1(s, fv_dst_fn, aux_only=False):
            wv = wk = None
            if fv_dst_fn is not None:
                nat = [load_w(w_cols(wb_in, 2048 + q * 256, 256)[0], (8, 256)) for q in range(2)]
                wv, wk = [], []
                for qp in range(2):
                    i = rr["w"]
                    rr["w"] = (i + 1) % NSLOT
                    pv_ = wslot[i][:, 0:2048].rearrange("p (k c) -> p k c", c=256)
                    pkey = "wslot%d" % i
                    for sq in range(2):
                        o_ = pv_.rearrange("p k (c g) -> p k c g", g=64)[:, :, :, sq * 32:(sq + 1) * 32]
                        i_ = nat[sq][0].rearrange("p k (g c) -> p k c g", c=8)[:, :, 4 * qp:4 * qp + 4, :]
                        cp("pool", o_, i_, reads=[nat[sq][1]], writes=[pkey])
                    wv.append(pv_)
                    wk.append(pkey)
            items = [dict(src=xh[s, :, :], npart=16, dst=xn_halo_dst, post=None)]
            for t in range(NT if not aux_only else 0):
                items.append(dict(src=xs[s, t * 128:(t + 1) * 128, :], npart=128, dst=xn_dst_t(t),
                                  post=(fv_post(t, t, wv, wk, fv_dst_fn) if fv_dst_fn is not None else None)))
            for mt in range(2):
                items.append(dict(src=mem[s, mt * 128:(mt + 1) * 128, :], npart=128, dst=mn_dst_fn(mt), post=None))
            tile_pipeline(items)
            for hp in range(2):
                wv, wk = load_w(w_cols(wb_kv, hp * 256, 256)[0], (8, 256))
                for hh in range(2):
                    h = 2 * hp + hh
                    ps, pk = nextF()
                    for k in range(8):
                        mm(ps[:, 0:256], wv[:, k, hh * 128:(hh + 1) * 128], mnT[:, k, :], k == 0, k == 7, reads=["mnT", wk], writes=[pk])
                    act(KT[:, h, :], ps[:, 0:256], AF.Copy, reads=[pk], writes=["KT"])
            wq_ = [load_w(w_cols(wb_kv, 512 + q * 256, 256)[0], (8, 256)) for q in range(2)]
            for mt in range(2):
                ps, pk = nextF()
                for q in range(2):
                    for k in range(8):
                        mm(ps[:, q * 256:(q + 1) * 256], mnT[:, k, mt * 128:(mt + 1) * 128], wq_[q][0][:, k, :], k == 0, k == 7,
                           reads=["mnT", wq_[q][1]], writes=[pk])
                act(Vv[:, mt, :], ps[:, :], AF.Copy, reads=[pk], writes=["Vv"])

        def fourier_groups(v, lhs_fn, out_fn, kbr=None, nB=256, dkey="Dt"):
            rr["nF"] = 4
            rr["F"] = 0
            if kbr is None:
                kbr = (kb[:, v, 0, :], kb[:, v, 1, :])
            pa = {}

            def fa(g):
                ps, pk = nextF()
                mm(ps[:, 0:256], lhs_fn(g), dfta[:, :], True, True, reads=[dkey, "const"], writes=[pk])
                pa[g] = (ps, pk)

            def ft1(g):
                ps, pk = pa[g]
                A2 = ps[:, 0:256].rearrange("p (a b) -> p a b", a=2)
                w4, k4 = tw4[g % 3], "tw4_%d" % (g % 3)
                tt("dve", w4[:, 0:2, :], A2, twid[:, v, 0, :, :], ALU.mult, reads=[pk, "const"], writes=[k4 + "a"])
                tt("dve", w4[:, 2:4, :], A2, twid[:, v, 1, :, :], ALU.mult, reads=[pk, "const"], writes=[k4 + "c"])

            def ft2(g):
                w4, k4 = tw4[g % 3], "tw4_%d" % (g % 3)
                yk = "Yt%d" % (g % 3)
                w4p = w4[:, :, :].rearrange("p (a b) c -> p a b c", b=2)
                tt("pool", Yt[g % 3][:, :, :], w4p[:, :, 0, :], w4p[:, :, 1, :], ALU.add, reads=[k4 + "a", k4 + "c"], writes=[yk + "r", yk + "i"])

            def fb(g):
                yk = "Yt%d" % (g % 3)
                ps2, pk2 = nextF()
                mm(ps2[:, 0:nB], Yt[g % 3][:, 0, :], kbr[0], True, False, reads=[yk + "r", "const"], writes=[pk2])
                mm(ps2[:, 0:nB], Yt[g % 3][:, 1, :], kbr[1], False, True, reads=[yk + "i", "const"], writes=[pk2])
                out_fn(g, ps2, pk2)

            for i in range(64 + 3):
                if i < 64:
                    fa(i)
                if 0 <= i - 1 < 64:
                    ft1(i - 1)
                if 0 <= i - 2 < 64:
                    ft2(i - 2)
                if 0 <= i - 3 < 64:
                    fb(i - 3)

        def f_finish(v, tile_fn, t0):
            for tl in range(4):
                hap, fkeys = tile_fn(t0 + tl)
                pt, pk = nextT()
                for h in range(4):
                    for ri in range(2):
                        for (iap, p0, npp) in hap(ri, h):
                            S.op("pe", lambda e, pt=pt, h=h, ri=ri, iap=iap, p0=p0, npp=npp:
                                 e.matmul(pt[p0:p0 + npp, (h * 2 + ri) * 128:(h * 2 + ri + 1) * 128], lhsT=iap, rhs=ident[:, :], start=True, stop=True),
                                 reads=fkeys + ["const"], writes=pk)
                cp(("dve", "act")[tl % 2], FTt[:, :, tl * 128:(tl + 1) * 128], pt[:, :].rearrange("p (a c) -> p a c", c=128), reads=pk, writes=["FTt%d" % tl])
            for h in range(4):
                ps, pk = nextF()
                mm(ps[:, :], fold[:, v, h, 0, :], FTt[:, h * 2, :], True, False, reads=["fold"] + ["FTt%d" % i for i in range(4)], writes=[pk])
                mm(ps[:, :], fold[:, v, h, 1, :], FTt[:, h * 2 + 1, :], False, True, reads=["fold"] + ["FTt%d" % i for i in range(4)], writes=[pk])
                cp("act", GT[:, h, t0 * 128:(t0 + 4) * 128], ps[:, :], reads=[pk], writes=["GT_%d" % (t0 // 4)])

        Dv16 = Dt[:, :].rearrange("p (i c) -> p i c", c=512)
        Dv128 = Dt[:, :].rearrange("p (i c) -> p i c", c=64)
        Fp = Fsb[:, :].rearrange("p (r k c) -> p r k c", r=2, k=16)
        Fs = Fsb[:, :].rearrange("p (r k c) -> p r k c", r=2, k=128)

        def fourier_prompt(s):
            dma("sp", Dt[:, :], fvS[s, :, :].rearrange("(p i) c -> p (i c)", i=16), reads=["fvS%d_%d" % (s, t) for t in range(NT)], writes=["Dt"], semkey="Dt")

            def out_fn(g, ps2, pk2):
                cp("act", Fp[:, :, :, 8 * g:8 * g + 8], ps2[:, 0:256].rearrange("p (r k c) -> p r k c", r=2, k=16), reads=[pk2], writes=["Fsb"])

            fourier_groups(0, lambda g: Dt[:, :].rearrange("p (m g) -> p m g", g=64)[:, :, g], out_fn)
            for t0 in range(0, NT, 4):
                f_finish(0, lambda t: ((lambda ri, h, t=t: [(Fp[:, ri, t, h * 128:(h + 1) * 128], 0, 128)]), ["Fsb"]), t0)

        def sample_fv_all():
            wq_ = [load_w(w_cols(wb_in, 2048 + q * 256, 256)[0], (8, 256), dep="wcast_fv") for q in range(2)]
            wv, wk = [x_[0] for x_ in wq_], [x_[1] for x_ in wq_]

            def store(t, tile, key):
                dma("act", G1[t * 128:(t + 1) * 128, :], tile[:, :], reads=[key], writes=["G1_%d" % t], semkey="fvst%d" % (t % 2))

            items = []
            for t in range(128):
                tq = t % NT
                items.append(dict(src=xfull[t * 128:(t + 1) * 128, :], npart=128, dst=xn_dst_t(tq), post=fv_post(t, tq, wv, wk, store)))
            tile_pipeline(items, extra=lambda i: issue_casts(1) if (i >= 8 and i % 2 == 0) else None)
            issue_casts(1000)
            dma("sp", wpg[:, :, :], wb_pg[:, :].rearrange("(k p) c -> p k c", p=128), reads=["wcast"], writes=["wpg"], semkey="wpg")

        def fourier_sample_local():
            Fp64 = Fp[:, :, :, :].rearrange("p r k (b c) -> p r k b c", c=64)
            Dbuf = [(Dt, "Dt"), (Dt2, "Dt2")]

            def load_blk(b):
                db, dk = Dbuf[b % 2]
                g1v = G1[:, :].rearrange("(p i) c -> p i c", i=128)
                dbv = db[:, :].rearrange("p (i c) -> p i c", c=64)
                for i0 in range(0, 128, 32):
                    dma("sp", dbv[:, i0:i0 + 32, :], g1v[:, i0:i0 + 32, b * 64:(b + 1) * 64],
                        reads=["G1_%d" % t for t in range(128)], writes=[dk], semkey=dk)

            load_blk(0)
            for b in range(8):
                if b + 1 < 8:
                    load_blk(b + 1)
                db, dk = Dbuf[b % 2]
                dv = db[:, :].rearrange("p (i c) -> p i c", c=64)

                def out_fn(g, ps2, pk2, b=b):
                    cp("act", Fp[:, :, :, 64 * b + g], ps2[:, 0:32].rearrange("p (r k) -> p r k", r=2), reads=[pk2], writes=["Fsb"])

                fourier_groups(1, lambda g, dv=dv: dv[:, :, g], out_fn, kbr=(kbo[:, 0, :], kbo[:, 1, :]), nB=32, dkey=dk)
            S.fence()
            for t0 in range(0, NT, 4):
                f_finish(1, lambda t: ((lambda ri, h, t=t: [(Fp[:, ri, t, h * 128:(h + 1) * 128], 0, 128)]), ["Fsb"]), t0)

        WIN_OFF = {"pv": 0, "pg": 1024, "fv": 2048, "fg": 2560, "q": 3072, "ag": 3584, "mg": 4096}

        wpref = {}

        def proj_fm(col0, nchunk, c0, evac_fn, xkeys):
            done = 0
            while done < nchunk:
                n = min(2, nchunk - done)
                if (col0 + done * 128) in wpref:
                    wv, wk = wpref.pop(col0 + done * 128)
                else:
                    wv, wk = load_w(w_cols(wb_in, col0 + done * 128, n * 128)[0], (8, n * 128))
                for j in range(n):
                    ps, pk = nextF()
                    for k in range(8):
                        mm(ps[:, :], wv[:, k, j * 128:(j + 1) * 128], xnT[:, k, c0:c0 + CH], k == 0, k == 7, reads=xkeys + [wk], writes=[pk])
                    evac_fn(done + j, ps, pk, wv, wk)
                done += n

        wo_pref = [None]

        def pass2_chunk(s, j, stage):
            c0 = 8 + j * CH
            xkeys = ["xnT_t%d" % t for t in range(4 * j, 4 * j + 4)]
            hkeys = xkeys + ["xnT_hl", "xnT_hr"] + (["xnT_t%d" % (4 * j - 1)] if j > 0 else []) + (["xnT_t%d" % (4 * j + 4)] if j < 3 else [])
            SCL = 1.0 / math.sqrt(128.0)
            if stage == "P":
                if j > 0:
                    cp("act", pvh[:, :, 0:16], pvsave[:, :, :], reads=["pvsave"], writes=["pvh%d" % i_ for i_ in range(8)])

                def pv_evac(i, ps, pk, wv, wk):
                    cp("act", pvh[:, i, 16:16 + CH], ps[:, :], reads=[pk], writes=["pvh%d" % i])
                    if j == 0:
                        ph, phk = nextF()
                        for k in range(8):
                            mm(ph[:, 0:16], wv[:, k, (i % 2) * 128:(i % 2 + 1) * 128], xnT[:, k, c0 - 8:c0 + 8], k == 0, k == 7,
                               reads=hkeys + [wk], writes=[phk])
                        cp("act", pvh[:, i, 0:16], ph[:, 0:16], reads=[phk], writes=["pvh%d" % i])

                proj_fm(WIN_OFF["pv"], 8, c0 + 8, pv_evac, hkeys)
                if j < 3:
                    cp("act", pvsave[:, :, :], pvh[:, :, CH:CH + 16], reads=["pvh%d" % i_ for i_ in range(8)], writes=["pvsave"])
                proj_fm(WIN_OFF["pg"], 8, c0, lambda i, ps, pk, wv, wk: act(pgs[:, i, :], ps[:, :], AF.Silu, reads=[pk], writes=["pgs%d" % i]), xkeys)
                W = CH + 16
                for g, w in enumerate((2, 4, 8, 16)):
                    u = pvh[:, 2 * g:2 * g + 2, :]
                    uk = ["pvh%d" % (2 * g), "pvh%d" % (2 * g + 1)]
                    tt("dve", tA[:, :, 1:W], u[:, :, 1:W], u[:, :, 0:W - 1], ALU.add, reads=uk, writes=["tA"])
                    fin, fk = tA, "tA"
                    if w >= 4:
                        if w == 4:
                            tt("dve", tB[:, :, 8:8 + CH], tA[:, :, 9:9 + CH], tA[:, :, 7:7 + CH], ALU.add, reads=["tA"], writes=["tB"])
                            fin, fk = tB, "tB"
                        else:
                            tt("dve", tB[:, :, 3:W], tA[:, :, 3:W], tA[:, :, 1:W - 2], ALU.add, reads=["tA"], writes=["tB"])
                            if w == 8:
                                tt("dve", tC[:, :, 8:8 + CH], tB[:, :, 11:11 + CH], tB[:, :, 7:7 + CH], ALU.add, reads=["tB"], writes=["tC"])
                                fin, fk = tC, "tC"
                            else:
                                tt("dve", tC[:, :, 7:W], tB[:, :, 7:W], tB[:, :, 3:W - 4], ALU.add, reads=["tB"], writes=["tC"])
                                tt("dve", tA[:, :, 8:8 + CH], tC[:, :, 15:15 + CH], tC[:, :, 7:7 + CH], ALU.add, reads=["tC"], writes=["tA"])
                                fin, fk = tA, "tA"
                    pk_ = ["pT%d" % (2 * g), "pT%d" % (2 * g + 1)]
                    stt("dve", pT[:, 2 * g:2 * g + 2, :], fin[:, :, 8:8 + CH], 1.0 / w, u[:, :, 8:8 + CH], ALU.mult, ALU.subtract, reads=[fk] + uk, writes=pk_)
                    for edge, (m0, e0) in enumerate(((8, 0), (8 + CH - 8, 8))):
                        if (edge == 0 and j == 0) or (edge == 1 and j == 3):
                            ic = icn[:, s, g, e0:e0 + 8].unsqueeze(1).to_broadcast([128, 2, 8])
                            tt("dve", fin[:, :, m0:m0 + 8], fin[:, :, m0:m0 + 8], ic, ALU.mult, reads=[fk, "const"], writes=[fk])
                            tt("dve", pT[:, 2 * g:2 * g + 2, m0 - 8:m0], fin[:, :, m0:m0 + 8], u[:, :, m0:m0 + 8], ALU.subtract, reads=[fk] + uk, writes=pk_)
            if stage == "G":
                for g in range(4):
                    pk_ = ["pT%d" % (2 * g), "pT%d" % (2 * g + 1)]
                    for mo in range(2):
                        ps, pk = nextF()
                        for ki in range(2):
                            mm(ps[:, :], wpg[:, 2 * g + ki, mo * 128:(mo + 1) * 128], pT[:, 2 * g + ki, :], ki == 0, ki == 1, reads=pk_ + ["wpg"], writes=[pk])
                        ci = 2 * g + mo
                        stt("dve", gaT[:, ci, :], ps[:, :], NCOL(16 + ci), pgs[:, ci, :], ALU.mult, ALU.mult, reads=[pk, "const", "pgs%d" % ci], writes=["gaT"])
            if stage == "A":
                proj_fm(WIN_OFF["q"], 4, c0, lambda i, ps, pk, wv, wk: act(qT[:, i, :], ps[:, :], AF.Copy, reads=[pk], writes=["qT"]), xkeys)
                proj_fm(WIN_OFF["ag"], 4, c0, lambda i, ps, pk, wv, wk: act(ags[:, i, :], ps[:, :], AF.Silu, reads=[pk], writes=["ags"]), xkeys)
                def att_scores(h):
                    es = h % 2
                    ek = "E%d" % es
                    for mt in range(2):
                        ps, pk = nextF()
                        mm(ps[:, :], KT[:, h, mt * 128:(mt + 1) * 128], qT[:, h, :], True, True, reads=["KT", "qT"], writes=[pk])
                        act(Ee[es][:, mt, :], ps[:, :], AF.Exp, reads=[pk], writes=[ek + "_%d" % mt], scale=SCL)

                def att_pv(h):
                    es = h % 2
                    ek = "E%d" % es
                    po, pok = nextF()
                    for mt in range(2):
                        mm(po[:, :], Vv[:, mt, h * 128:(h + 1) * 128], Ee[es][:, mt, :], mt == 0, mt == 1, reads=["Vv", ek + "_%d" % mt], writes=[pok])
                    pd, pdk = nextF()
                    for mt in range(2):
                        mm(pd[:, :], ones[:, :], Ee[es][:, mt, :], mt == 0, mt == 1, reads=["ones", ek + "_%d" % mt], writes=[pdk])
                    S.op("dve", lambda e, pd=pd: e.reciprocal(out=rden[:, :], in_=pd[:, :]), reads=[pdk], writes=["rden"])
                    tt("dve", otmp[:, :], po[:, :], rden[:, :], ALU.mult, reads=[pok, "rden"], writes=["otmp"])
                    tt("pool", gcT[:, h, :], otmp[:, :], ags[:, h, :], ALU.mult, reads=["otmp", "ags"], writes=["gcT"])

                att_scores(0)
                for h in range(4):
                    if h + 1 < 4:
                        att_scores(h + 1)
                    att_pv(h)
                def fg_evac(i, ps, pk, wv, wk):
                    act(fgs[:, i, :], ps[:, :], AF.Silu, reads=[pk], writes=["fgs%d" % i])
                    tt("dve", gbT[:, i, :], GT[:, i, j * CH:(j + 1) * CH], fgs[:, i, :], ALU.mult, reads=["GT_%d" % j, "fgs%d" % i], writes=["gbT"])

                proj_fm(WIN_OFF["fg"], 4, c0, fg_evac, xkeys)
            if stage == "M":
                for mp in range(4):
                    wg = [load_w(w_cols(wb_in, 4096 + b * 1024 + mp * 256, 256)[0], (8, 256)) for b in range(3)]
                    wpo_, kpo = load_w(w_cols(wb_po, mp * 256, 256)[0], (8, 256))
                    i = rr["w"]
                    rr["w"] = (i + 1) % NSLOT
                    wfa = wslot[i][:, 0:2048].rearrange("p (k c) -> p k c", c=256)
                    kfa = "wslot%d" % i
                    dma("sp", wfa[:, 0:4, :], wb_fo[mp, :, :].rearrange("p (k c) -> p k c", k=4), reads=["wcast"], writes=[kfa], semkey=kfa)
                    dma("sp", wfa[:, 4:8, :], wb_ao[mp, :, :].rearrange("p (k c) -> p k c", k=4), reads=["wcast"], writes=[kfa], semkey=kfa)
                    for mm_ in range(2):
                        m = 2 * mp + mm_
                        cs = slice(mm_ * 128, (mm_ + 1) * 128)
                        gs_ = m % 2
                        for b in range(3):
                            wg_, wgk = wg[b]
                            ps, pk = nextF()
                            for k in range(8):
                                mm(ps[:, :], wg_[:, k, cs], xnT[:, k, c0:c0 + CH], k == 0, k == 7, reads=xkeys + [wgk], writes=[pk])
                            act(gate[gs_][:, b, :], ps[:, :], AF.Sigmoid, reads=[pk, "const"], writes=["gate%d_%d" % (gs_, b)], bias=NCOL(24 + b * 8 + m))
                        pa, pak = nextF()
                        for k in range(8):
                            mm(pa[:, :], wpo_[:, k, cs], gaT[:, k, :], k == 0, k == 7, reads=["gaT", kpo], writes=[pak])
                        tt("dve", m1[:, :], pa[:, :], gate[gs_][:, 0, :], ALU.mult, reads=[pak, "gate%d_0" % gs_], writes=["m1"])
                        pb, pbk = nextF()
                        for k in range(4):
                            mm(pb[:, :], wfa[:, k, cs], gbT[:, k, :], k == 0, k == 3, reads=["gbT", kfa], writes=[pbk])
                        tt("dve", m2[:, :], pb[:, :], gate[gs_][:, 1, :], ALU.mult, reads=[pbk, "gate%d_1" % gs_], writes=["m2"])
                        pc, pck = nextF()
                        for k in range(4):
                            mm(pc[:, :], wfa[:, 4 + k, cs], gcT[:, k, :], k == 0, k == 3, reads=["gcT", kfa], writes=[pck])
                        tt("dve", m3[:, :], pc[:, :], gate[gs_][:, 2, :], ALU.mult, reads=[pck, "gate%d_2" % gs_], writes=["m3"])
                        tt("pool", m1[:, :], m1[:, :], m2[:, :], ALU.add, reads=["m1", "m2"], writes=["m1"])
                        tt("pool", mT[:, m, :], m1[:, :], m3[:, :], ALU.add, reads=["m1", "m3"], writes=["mT"])
                wo_pref[0] = [load_w(w_cols(wb_o, q * 256, 256)[0], (8, 256)) for q in range(4)]
            if stage == "O":
                wo = wo_pref[0]
                info = {}
                if j + 1 < 4:
                    for col_ in (WIN_OFF["q"], WIN_OFF["q"] + 256, WIN_OFF["ag"], WIN_OFF["ag"] + 256):
                        wpref[col_] = load_w(w_cols(wb_in, col_, 256)[0], (8, 256))

                def o_main(tl):
                    t = 4 * j + tl
                    sl = next_x()
                    xk = "xin%d" % sl
                    dma("sp", xin[sl][:, :], xs[s, t * 128:(t + 1) * 128, :], reads=[], writes=[xk], semkey=xk)
                    hs = t % 2
                    hk = "hout%d" % hs
                    S.op("pool", lambda e, hs=hs: e.memset(stat[:, 4 + hs, 0:2], 0.0), writes=["sf%dssa" % hs])
                    for hf in range(2):
                        ps, pk = nextF()
                        for qq in range(2):
                            wv, wk = wo[2 * hf + qq]
                            for k in range(8):
                                mm(ps[:, qq * 256:(qq + 1) * 256], mT[:, k, tl * 128:(tl + 1) * 128], wv[:, k, :], k == 0, k == 7, reads=["mT", wk], writes=[pk])
                        tt("dve", hout[hs][:, hf * 512:(hf + 1) * 512], ps[:, :], xin[sl][:, hf * 512:(hf + 1) * 512], ALU.add, reads=[pk, xk], writes=[hk])
                        act(xin[sl][:, hf * 512:(hf + 1) * 512], hout[hs][:, hf * 512:(hf + 1) * 512], AF.Square, reads=[hk], writes=[xk, "sf%dssa" % hs],
                            accum_out=stat[:, 4 + hs, hf:hf + 1])
                    tt("dve", stat[:, 4 + hs, 2:3], stat[:, 4 + hs, 0:1], stat[:, 4 + hs, 1:2], ALU.add, reads=["sf%dssa" % hs], writes=["sf%dss" % hs])
                    ts("dve", stat[:, 4 + hs, 3:4], stat[:, 4 + hs, 2:3], 1.0 / D, EPS, ALU.mult, ALU.add, reads=["sf%dss" % hs], writes=["sf%dms" % hs])
                    tt("pool", stat[:, 4 + hs, 4:5], stat[:, 4 + hs, 3:4], neghalf[:, :], ALU.pow, reads=["sf%dms" % hs, "neghalf"], writes=["sf%dy" % hs])

                def o_fin(tl):
                    t = 4 * j + tl
                    hs = t % 2
                    hk = "hout%d" % hs
                    key = "sf%d" % hs
                    ms_ap, y_ap, t_ap = stat[:, 4 + hs, 3:4], stat[:, 4 + hs, 4:5], stat[:, 4 + hs, 5:6]
                    for _ in range(2):
                        tt("dve", t_ap, y_ap, y_ap, ALU.mult, reads=[key + "y"], writes=[key + "t"])
                        tt("dve", t_ap, t_ap, ms_ap, ALU.mult, reads=[key + "t", key + "ms"], writes=[key + "t"])
                        ts("dve", t_ap, t_ap, -0.5, 1.5, ALU.mult, ALU.add, reads=[key + "t"], writes=[key + "t"])
                        tt("dve", y_ap, y_ap, t_ap, ALU.mult, reads=[key + "y", key + "t"], writes=[key + "y"])
                    stt("dve", hout[hs][:, :], hout[hs][:, :], y_ap, nfr[:, :], ALU.mult, ALU.mult, reads=[hk, key + "y", "const"], writes=[hk])
                    dma("act", yout[s, t * 128:(t + 1) * 128, :], hout[hs][:, :], reads=[hk], writes=[], semkey="yst%d" % hs)

                for tl in range(5):
                    if tl < 4:
                        o_main(tl)
                    if tl >= 1:
                        o_fin(tl - 1)

        def fv_to_scratch(s):
            def f(t, tile, key):
                dma("act", fvS[s, t * 128:(t + 1) * 128, :], tile[:, :], reads=[key], writes=["fvS%d_%d" % (s, t)], semkey="fvst%d" % (t % 2))
            return f

        def pass2_segment(s):
            rr["nF"] = 8
            S.rgn_keys = set(["pvh%d" % i for i in range(8)] + ["tA", "tB", "tC", "qT", "ags", "rden", "otmp"]
                             + ["E%d_%d" % (a_, b_) for a_ in range(2) for b_ in range(2)] + ["fgs%d" % i for i in range(4)])
            marker = lambda: S.op("pool", lambda e: e.memset(stat[:, 7, 0:1], 0.0), writes=["RGN"])
            marker()
            pass2_chunk(s, 0, "P")
            pass2_chunk(s, 0, "G")
            for j in range(4):
                marker()
                pass2_chunk(s, j, "A")
                marker()
                if j + 1 < 4:
                    pass2_chunk(s, j + 1, "P")
                pass2_chunk(s, j, "M")
                if j + 1 < 4:
                    pass2_chunk(s, j + 1, "G")
                pass2_chunk(s, j, "O")
            S.rgn_keys = set()

        sample_fv_all()
        S.fence()
        fourier_sample_local()
        S.fence()
        pass1(2, None, aux_only=True)
        S.fence()
        pass2_segment(2)
        S.fence()
        for s_ in range(2):
            pass1(s_, fv_to_scratch(s_))
            S.fence()
            fourier_prompt(s_)
            S.fence()
            pass2_segment(s_)
            S.fence()

        S.finalize()
        S.emit({"act": ["yst0", "yst1"]})
    return nc


def _tables(r=0):
    bf = ml_dtypes.bfloat16
    p = np.arange(128, dtype=np.float64)
    ang = 2 * np.pi * np.outer(p, p) / 128.0
    C, Sn = np.cos(ang), np.sin(ang)
    dfta = np.concatenate([C, -Sn], axis=1).astype(np.float32).astype(bf)
    k = np.arange(128, dtype=np.float64)
    tw = np.zeros((128, 2, 2, 2, 128), np.float64)
    i_p = (np.arange(128) // 8).astype(np.float64)
    rot = 2 * np.pi * np.outer(np.ones(128), k) * (r + 1) / 8.0
    for v_, a_ in enumerate((2 * np.pi * np.outer(i_p, k) / 2048.0, 2 * np.pi * np.outer(p, k) / 16384.0 + rot)):
        tw[:, v_, 0, 0], tw[:, v_, 0, 1] = np.cos(a_), np.sin(a_)
        tw[:, v_, 1, 0], tw[:, v_, 1, 1] = -np.sin(a_), np.cos(a_)
    kbm = np.zeros((128, 2, 2, 256), np.float64)
    i16 = np.arange(16, dtype=np.float64)
    a = 2 * np.pi * np.outer(i16, i16) / 16.0
    kc = np.kron(np.cos(a), np.eye(8))
    ks = np.kron(-np.sin(a), np.eye(8))
    kbm[:, 0, 0] = np.concatenate([kc, ks], axis=1)
    kbm[:, 0, 1] = np.concatenate([-ks, kc], axis=1)
    kbm[:, 1, 0] = np.concatenate([C, -Sn], axis=1)
    kbm[:, 1, 1] = np.concatenate([Sn, C], axis=1)
    cs = np.concatenate([C, Sn], axis=1).astype(np.float32)
    return {
        "t_ident": np.eye(128, dtype=np.float32).astype(bf),
        "t_dfta": dfta,
        "t_twid": tw.reshape(128, -1).astype(np.float32),
        "t_kb": kbm.reshape(128, -1).astype(np.float32).astype(bf),
        "t_cs": cs,
    }


def _kbo(r):
    i = np.arange(128, dtype=np.float64)
    kh = np.arange(16 * r, 16 * r + 16, dtype=np.float64)
    a = 2 * np.pi * np.outer(i, kh) / 128.0
    kc, ks = np.cos(a), -np.sin(a)
    m = np.concatenate([kc, ks, -ks, kc], axis=1)
    return m.astype(np.float32).astype(ml_dtypes.bfloat16)


def _invcnt():
    out = np.zeros((NCORES, 3, 4, 16), np.float32)
    for r in range(NCORES):
        for s in range(3):
            if s < 2:
                S_, base = SEG, 0
            else:
                S_, base = 16384, SEG * r
            for g, w in enumerate((2, 4, 8, 16)):
                half = w // 2
                for e in range(16):
                    t = base + (e if e < 8 else SEG - 16 + e)
                    lo = min(max(t - half, 0), S_)
                    hi = min(max(t + half, 0), S_)
                    out[r, s, g, e] = 1.0 / float(hi - lo)
    return out


_NC_CACHE = {}


def kernel(x_prompt, x_sample, mem_prompt, mem_sample, norm_in, norm_mem, w_in, w_pool_grp,
           pool_scale, w_four_grp, w_kv, w_pool_out, w_four_out, w_attn_out, b_gate, w_o, norm_f):
    f = lambda a: np.ascontiguousarray(np.asarray(a), dtype=np.float32)
    x_prompt, x_sample, mem_prompt, mem_sample = f(x_prompt), f(x_sample), f(mem_prompt), f(mem_sample)
    tabs = _tables()
    tabs.pop("t_twid")
    icnt = _invcnt()
    cols = np.concatenate([
        f(norm_in)[0].reshape(8, 128).T, f(norm_mem)[0].reshape(8, 128).T,
        f(pool_scale)[0].reshape(8, 128).T, f(b_gate)[0].reshape(24, 128).T], axis=1)
    shared = {
        "w_in": f(w_in)[0], "w_kv": f(w_kv)[0], "w_po": f(w_pool_out)[0], "w_fo": f(w_four_out)[0],
        "w_ao": f(w_attn_out)[0], "w_o": f(w_o)[0], "w_pg": f(w_pool_grp)[0].reshape(1024, 256),
        "w_fg": np.ascontiguousarray(f(w_four_grp)[0].transpose(1, 0, 2)),
        "cols": np.ascontiguousarray(cols), "nf_row": f(norm_f).reshape(1, D),
    }
    shared.update(tabs)
    in_maps = []
    zeros8 = np.zeros((8, D), np.float32)
    for r in range(NCORES):
        xs = np.stack([x_prompt[2 * r], x_prompt[2 * r + 1], x_sample[0, SEG * r:SEG * (r + 1)]])
        xh = np.zeros((3, 16, D), np.float32)
        xh[2, 0:8] = x_sample[0, SEG * r - 8:SEG * r] if r > 0 else zeros8
        xh[2, 8:16] = x_sample[0, SEG * (r + 1):SEG * (r + 1) + 8] if r < NCORES - 1 else zeros8
        mem = np.stack([mem_prompt[2 * r], mem_prompt[2 * r + 1], mem_sample[0]])
        m = {"xs": xs, "xh": xh, "mem": mem, "icnt": icnt[r].reshape(1, -1), "t_kbo": _kbo(r),
             "xfull": np.ascontiguousarray(np.roll(x_sample[0], -SEG * (r + 1), axis=0)), "t_twid": _tables(r)["t_twid"]}
        m.update(shared)
        in_maps.append(m)
    if "nc" not in _NC_CACHE:
        _NC_CACHE["nc"] = build_program()
    res = run_bass_kernel_spmd(_NC_CACHE["nc"], in_maps, core_ids=list(range(NCORES)))
    y_prompt = np.empty((16, SEG, D), np.float32)
    y_sample = np.empty((1, 16384, D), np.float32)
    for r in range(NCORES):
        yo = res.results[r]["yout"]
        y_prompt[2 * r] = yo[0]
        y_prompt[2 * r + 1] = yo[1]
        y_sample[0, SEG * r:SEG * (r + 1)] = yo[2]
    return (y_prompt, y_sample)
```

```python
import math
from contextlib import ExitStack

import numpy as np
import ml_dtypes

import concourse.bass as bass
import concourse.mybir as mybir
from concourse.bass_utils import run_bass_kernel_spmd

F32 = mybir.dt.float32
BF16 = mybir.dt.bfloat16
AF = mybir.ActivationFunctionType
ALU = mybir.AluOpType

NCORES = 8
D = 1024
SEG = 2048
NT = 16
CH = 512
EPS = 1e-6
ENGS = ("pe", "act", "dve", "pool", "sp")


class _Op:
    __slots__ = ("eng", "fn", "deps", "dma", "semkey", "need_inc", "val", "waits", "grp_total", "inc")

    def __init__(self, eng, fn, deps, dma, semkey, grp_total, inc=16):
        self.eng, self.fn, self.deps, self.dma, self.semkey = eng, fn, deps, dma, semkey
        self.inc = inc
        self.need_inc = False
        self.val = None
        self.waits = []
        self.grp_total = grp_total


class Sched:
    def __init__(self, nc):
        self.nc = nc
        self.ops = []
        self.last_w = {}
        self.reads = {}
        self.fence_ops = set()
        self.last_eng = {}
        self.open_dma = []
        self.rgn_keys = set()

    def op(self, eng, fn, reads=(), writes=(), dma=False, semkey=None, grp_total=False, inc=16, nofence=False):
        if self.rgn_keys and (self.rgn_keys & (set(reads) | set(writes))) and "RGN" not in writes:
            reads = list(reads) + ["RGN"]
        deps = set() if nofence else set(self.fence_ops)
        for k in reads:
            w = self.last_w.get(k)
            if w is not None:
                deps.add(w)
        for k in writes:
            w = self.last_w.get(k)
            if w is not None:
                deps.add(w)
            deps.update(self.reads.get(k, ()))
        o = _Op(eng, fn, deps, dma, semkey, grp_total, inc)
        self.ops.append(o)
        for k in reads:
            self.reads.setdefault(k, []).append(o)
        for k in writes:
            self.last_w[k] = o
            self.reads[k] = []
        if dma:
            self.open_dma.append(o)
        else:
            self.last_eng[eng] = o
        return o

    def fence(self):
        self.fence_ops = set(self.last_eng.values()) | set(self.open_dma)
        self.open_dma = []

    def finalize(self):
        for o in self.ops:
            o.deps = {d for d in o.deps if not (o.eng == "pe" and d.eng == "pe" and not d.dma and not o.dma)}
            for d in o.deps:
                if not d.dma:
                    d.need_inc = True
        eng_cnt = {e: 0 for e in ENGS}
        dma_cnt = {}
        for o in self.ops:
            if o.dma:
                dma_cnt[o.semkey] = dma_cnt.get(o.semkey, 0) + o.inc
                o.val = dma_cnt[o.semkey]
            elif o.need_inc:
                eng_cnt[o.eng] += 1
                o.val = eng_cnt[o.eng]
        for o in self.ops:
            if o.dma and o.grp_total:
                o.val = dma_cnt[o.semkey]
        known = {e: {} for e in ENGS}
        for o in self.ops:
            need = {}
            for d in o.deps:
                key = ("dma", d.semkey) if d.dma else ("eng", d.eng)
                if need.get(key, 0) < d.val:
                    need[key] = d.val
            kn = known[o.eng]
            for key, v in need.items():
                if kn.get(key, 0) >= v:
                    continue
                kn[key] = v
                o.waits.append((key, v))
        self.dma_keys = sorted(dma_cnt.keys(), key=str)
        self.dma_cnt = dma_cnt

    def emit(self, final_keys):
        nc = self.nc
        with ExitStack() as st:
            esem = {e: st.enter_context(nc.semaphore("s_" + e)) for e in ENGS}
            dsem = {k: st.enter_context(nc.semaphore("d%d" % i)) for i, k in enumerate(self.dma_keys)}
            block = st.enter_context(nc.Block())
            by_eng = {e: [o for o in self.ops if o.eng == e] for e in ENGS}

            def run(e, eng):
                for o in by_eng[e]:
                    for key, v in o.waits:
                        eng.wait_ge(dsem[key[1]] if key[0] == "dma" else esem[key[1]], v)
                    inst = o.fn(eng)
                    if o.dma:
                        inst.then_inc(dsem[o.semkey], o.inc)
                    elif o.need_inc:
                        inst.then_inc(esem[e], 1)
                for k in final_keys.get(e, ()):
                    eng.wait_ge(dsem[k], self.dma_cnt[k])

            block.tensor(lambda eng: run("pe", eng))
            block.scalar(lambda eng: run("act", eng))
            block.vector(lambda eng: run("dve", eng))
            block.gpsimd(lambda eng: run("pool", eng))
            block.sync(lambda eng: run("sp", eng))


def build_program():
    nc = bass.Bass("TRN2", target_bir_lowering=False)
    S = Sched(nc)

    def din(name, shape, dt=F32):
        return nc.dram_tensor(name, list(shape), dt, kind="ExternalInput")

    xs = din("xs", [3, SEG, D])
    xh = din("xh", [3, 16, D])
    mem = din("mem", [3, 256, D])
    xfull = din("xfull", [16384, D])
    t_kbo = din("t_kbo", [128, 64], BF16)
    w_in = din("w_in", [D, 7168])
    w_kv = din("w_kv", [D, 1024])
    w_po = din("w_po", [D, 1024])
    w_fo = din("w_fo", [512, 1024])
    w_ao = din("w_ao", [512, 1024])
    w_o = din("w_o", [D, 1024])
    w_pg = din("w_pg", [1024, 256])
    w_fg = din("w_fg", [128, 4, 128])
    cols = din("cols", [128, 48])
    nf_row = din("nf_row", [1, D])
    icnt = din("icnt", [1, 3 * 4 * 16])
    t_ident = din("t_ident", [128, 128], BF16)
    t_dfta = din("t_dfta", [128, 256], BF16)
    t_twid = din("t_twid", [128, 2 * 2 * 256])
    t_kb = din("t_kb", [128, 2 * 2 * 256], BF16)
    t_cs = din("t_cs", [128, 256])
    yout = nc.dram_tensor("yout", [3, SEG, D], F32, kind="ExternalOutput")

    wb_in = nc.dram_tensor("wb_in", [28, 128, 2048], BF16)
    wb_kv = nc.dram_tensor("wb_kv", [4, 128, 2048], BF16)
    wb_po = nc.dram_tensor("wb_po", [4, 128, 2048], BF16)
    wb_fo = nc.dram_tensor("wb_fo", [4, 128, 1024], BF16)
    wb_ao = nc.dram_tensor("wb_ao", [4, 128, 1024], BF16)
    wb_o = nc.dram_tensor("wb_o", [4, 128, 2048], BF16)
    wb_pg = nc.dram_tensor("wb_pg", [1024, 256], BF16)
    fvS = nc.dram_tensor("fvS", [2, SEG, 512], BF16)
    G1 = nc.dram_tensor("G1", [16384, 512], BF16)

    st = ExitStack()
    with st:
        def sb(name, shape, dt):
            return st.enter_context(nc.sbuf_tensor(name, list(shape), dt))

        ident = sb("ident", [128, 128], BF16)
        ones = sb("ones", [128, 128], BF16)
        dfta = sb("dfta", [128, 256], BF16)
        twid2 = sb("twid", [128, 1024], F32)
        twid = twid2[:, :].rearrange("p (a b c d) -> p a b c d", a=2, b=2, c=2)
        kb2 = sb("kb", [128, 1024], BF16)
        kb = kb2[:, :].rearrange("p (a b c) -> p a b c", a=2, b=2)
        kbo2 = sb("kbo", [128, 64], BF16)
        kbo = kbo2[:, :].rearrange("p (a b) -> p a b", a=2)
        csm2 = sb("csm", [128, 256], F32)
        csm = csm2[:, :].rearrange("p (a b) -> p a b", a=2)
        wfg2 = sb("wfg", [128, 512], F32)
        wfg = wfg2[:, :].rearrange("p (a b) -> p a b", a=4)
        fold = sb("fold", [128, 2, 4, 2, 128], BF16)
        colsb = sb("colsb", [128, 48], F32)
        nfr = sb("nfr", [128, D], F32)
        icn2 = sb("icn", [128, 192], F32)
        icn = icn2[:, :].rearrange("p (a b c) -> p a b c", a=3, b=4)
        wpg = sb("wpg", [128, 8, 256], BF16)
        neghalf = sb("neghalf", [128, 1], F32)
        pvsave = sb("pvsave", [128, 8, 16], BF16)
        xnT = sb("xnT", [128, 8, SEG + 16], BF16)
        GT = sb("GT", [128, 4, SEG], BF16)
        KT = sb("KT", [128, 4, 256], BF16)
        Vv = sb("Vv", [128, 2, 512], BF16)
        mnT = sb("mnT", [128, 8, 256], BF16)
        NSLOT = 9
        wslot = [sb("wslot%d" % i, [128, 2048], BF16) for i in range(NSLOT)]
        xin = [sb("xin%d" % i, [128, D], F32) for i in range(2)]
        hout = [sb("hout%d" % i, [128, D], F32) for i in range(2)]
        stat = sb("stat", [128, 8, 24], F32)
        ARENA = 37 * 1024
        arena = sb("arena", [128, ARENA], BF16)
        apos = [0]

        def carve(shape, dt):
            n = int(np.prod(shape))
            nb = n * (2 if dt == F32 else 1)
            a = arena[:, apos[0]:apos[0] + nb]
            apos[0] += nb
            assert apos[0] <= ARENA, apos[0]
            if dt == F32:
                a = a.bitcast(F32)
            if len(shape) == 2:
                return a.rearrange("p (a b) -> p a b", b=shape[1])
            if len(shape) == 3:
                return a.rearrange("p (a b c) -> p a b c", b=shape[1], c=shape[2])
            return a

        apos[0] = 0
        Dt = carve([16 * 512], BF16)
        Fsb = carve([2 * 16 * 512], BF16)
        Yt = [carve([2, 128], BF16) for _ in range(3)]
        tw4 = [carve([4, 128], BF16) for _ in range(3)]
        posX = apos[0]
        Dt2 = carve([16 * 512], BF16)
        apos[0] = posX
        xsb = [carve([D], BF16) for _ in range(2)]
        fvt = [carve([512], BF16) for _ in range(2)]
        FTt = carve([8, 512], BF16)
        endA = apos[0]
        apos[0] = 16 * 512
        xinP = list(xin) + [carve([D], F32) for _ in range(4)]
        xsbP = list(xsb) + [carve([D], BF16) for _ in range(2)]
        sqjunk = carve([D], BF16)
        apos[0] = 0
        gaT = carve([8, CH], BF16)
        gcT = carve([4, CH], BF16)
        gbT = carve([4, CH], BF16)
        gate = [carve([3, CH], BF16) for _ in range(2)]
        m1 = carve([CH], F32)
        m2 = carve([CH], F32)
        m3 = carve([CH], F32)
        mT = carve([8, CH], BF16)
        posR = apos[0]
        pvh = carve([8, CH + 16], BF16)
        tA = carve([2, CH + 16], F32)
        tB = carve([2, CH + 16], F32)
        tC = carve([2, CH + 16], F32)
        pT = carve([8, CH], BF16)
        pgs = carve([8, CH], BF16)
        endB1 = apos[0]
        apos[0] = posR
        qT = carve([4, CH], BF16)
        ags = carve([4, CH], BF16)
        Ee = [carve([2, CH], BF16) for _ in range(2)]
        rden = carve([CH], F32)
        otmp = carve([CH], F32)
        fgs = carve([4, CH], BF16)
        endB = max(endB1, apos[0])

        psW = [st.enter_context(nc.psum_tensor("psW%d" % i, [128, 1024], F32)) for i in range(4)]
        psF = [psW[i // 2][:, (i % 2) * 512:(i % 2 + 1) * 512] for i in range(8)]
        psT = [psW[2], psW[3]]
        rr = {"F": 0, "T": 0, "w": 0, "nF": 8}

        def nextF():
            i = rr["F"] % rr["nF"]
            rr["F"] = (i + 1) % rr["nF"]
            return psF[i], "psF%d" % i

        class _Wide:
            def __init__(self, t, keys):
                self.t, self.keys = t, keys

        def nextT():
            i = rr["T"]
            rr["T"] = 1 - i
            return psT[i], ["psF%d" % (4 + 2 * i), "psF%d" % (5 + 2 * i)]

        _pid = {}

        def getpid(e):
            k = id(e)
            if k not in _pid:
                _pid[k] = e.partition_id()
            return _pid[k]

        def dma(eng, out, in_, reads, writes, semkey, grp_total=False, nofence=False):
            return S.op(eng, lambda e, out=out, in_=in_: e.dma_start(out=out, in_=in_),
                        reads=reads, writes=writes, dma=True, semkey=semkey, grp_total=grp_total, nofence=nofence)

        def mm(out, lhsT, rhs, start, stop, reads, writes):
            return S.op("pe", lambda e, out=out, lhsT=lhsT, rhs=rhs, start=start, stop=stop:
                        e.matmul(out, lhsT=lhsT, rhs=rhs, start=start, stop=stop), reads=reads, writes=writes)

        def act(out, in_, func, reads, writes, bias=None, scale=None, accum_out=None):
            kw = {}
            if bias is not None:
                kw["bias"] = bias
            if scale is not None:
                kw["scale"] = scale
            if accum_out is not None:
                kw["accum_out"] = accum_out
            return S.op("act", lambda e, out=out, in_=in_, func=func, kw=kw: e.activation(out=out, in_=in_, func=func, **kw),
                        reads=reads, writes=writes)

        def tt(eng, out, in0, in1, op, reads, writes):
            return S.op(eng, lambda e, out=out, in0=in0, in1=in1, op=op: e.tensor_tensor(out=out, in0=in0, in1=in1, op=op),
                        reads=reads, writes=writes)

        def ts(eng, out, in0, s1, s2, op0, op1, reads, writes):
            if s2 is None:
                return S.op(eng, lambda e, out=out, in0=in0, s1=s1, op0=op0: e.tensor_scalar(out=out, in0=in0, scalar1=s1, scalar2=None, op0=op0),
                            reads=reads, writes=writes)
            return S.op(eng, lambda e, out=out, in0=in0, s1=s1, s2=s2, op0=op0, op1=op1:
                        e.tensor_scalar(out=out, in0=in0, scalar1=s1, scalar2=s2, op0=op0, op1=op1), reads=reads, writes=writes)

        def stt(eng, out, in0, scalar, in1, op0, op1, reads, writes):
            return S.op(eng, lambda e, out=out, in0=in0, scalar=scalar, in1=in1, op0=op0, op1=op1:
                        e.scalar_tensor_tensor(out=out, in0=in0, scalar=scalar, in1=in1, op0=op0, op1=op1), reads=reads, writes=writes)

        def cp(eng, out, in_, reads, writes):
            if eng == "act":
                return act(out, in_, AF.Copy, reads, writes)
            return S.op(eng, lambda e, out=out, in_=in_: e.tensor_copy(out=out, in_=in_), reads=reads, writes=writes)

        def load_w(dram_ap, shape3, dep="wcast"):
            i = rr["w"]
            rr["w"] = (i + 1) % NSLOT
            n = shape3[0] * shape3[1]
            v = wslot[i][:, 0:n].rearrange("p (k c) -> p k c", c=shape3[1])
            key = "wslot%d" % i
            dma("sp", v, dram_ap, reads=[dep], writes=[key], semkey=key, nofence=True)
            return v, key

        def load_l(li, dram_ap, shape3):
            n = shape3[0] * shape3[1]
            v = lslot[li][:, 0:n].rearrange("p (k c) -> p k c", c=shape3[1])
            key = "lslot%d" % li
            dma("sp", v, dram_ap, reads=["wcast"], writes=[key], semkey=key)
            return v, key

        def w_cols(wb, c0, ncol, kchunks=8):
            assert ncol == 256 and c0 % 256 == 0, (c0, ncol)
            return wb[c0 // 256, :, :].rearrange("p (k c) -> p k c", k=kchunks), (kchunks, ncol)

        def cast_blk(dst, src, cb, kk):
            return (dst[cb, :, :].rearrange("p (k c) -> p k c", k=kk), src[:, cb * 256:(cb + 1) * 256].rearrange("(k p) c -> p k c", p=128))

        for ci, cb in enumerate((8, 9)):
            o_, i_ = cast_blk(wb_in, w_in, cb, 8)
            dma("pool", o_, i_, reads=[], writes=(["wcast_fv"] if ci == 1 else []), semkey="wcast_fv", grp_total=True)
        casts = [cast_blk(wb_in, w_in, cb, 8) for cb in range(28) if cb not in (8, 9)]
        casts += [cast_blk(wb_kv, w_kv, cb, 8) for cb in range(4)]
        casts += [cast_blk(wb_po, w_po, cb, 8) for cb in range(4)]
        casts += [cast_blk(wb_fo, w_fo, cb, 4) for cb in range(4)]
        casts += [cast_blk(wb_ao, w_ao, cb, 4) for cb in range(4)]
        casts += [cast_blk(wb_o, w_o, cb, 8) for cb in range(4)]
        casts += [(wb_pg[:, :], w_pg[:, :])]
        cast_state = {"i": 0}

        def issue_casts(n):
            for _ in range(n):
                ci = cast_state["i"]
                if ci >= len(casts):
                    return
                o_, i_ = casts[ci]
                dma("pool", o_, i_, reads=[], writes=(["wcast"] if ci == len(casts) - 1 else []), semkey="wcast", grp_total=True)
                cast_state["i"] = ci + 1
        cl = lambda out, in_, last=False: dma("sp", out, in_, reads=[], writes=(["const"] if last else []), semkey="const", grp_total=True)
        cl(ident[:, :], t_ident[:, :])
        cl(dfta[:, :], t_dfta[:, :])
        cl(twid2[:, :], t_twid[:, :])
        cl(kb2[:, :], t_kb[:, :])
        cl(kbo2[:, :], t_kbo[:, :])
        cl(csm2[:, :], t_cs[:, :])
        cl(wfg2[:, :], w_fg[:, :, :].rearrange("p a b -> p (a b)"))
        cl(colsb[:, :], cols[:, :])
        cl(nfr[:, :], nf_row[0:1, :].partition_broadcast(128))
        cl(icn2[:, :], icnt[0:1, :].partition_broadcast(128), last=True)
        S.op("pool", lambda e: e.memset(ones[:, :], 1.0), writes=["ones"])
        S.op("pool", lambda e: e.memset(neghalf[:, :], -0.5), writes=["neghalf"])
        for h in range(4):
            for ab in range(2):
                ps, pk = nextF()
                mm(ps[:, 0:128], csm[:, ab, :], wfg[:, h, :], True, True, reads=["const"], writes=[pk])
                for v, Sq in enumerate((2048, 16384)):
                    act(fold[:, v, h, ab, :], ps[:, 0:128], AF.Copy, reads=[pk], writes=["fold"], scale=1.0 / math.sqrt(Sq * 128.0))

        NCOL = lambda k: colsb[:, k:k + 1]

        def rstd_from_ss(ss_ap, ms_ap, y_ap, t_ap, key):
            ts("dve", ms_ap, ss_ap, 1.0 / D, EPS, ALU.mult, ALU.add, reads=[key + "ss"], writes=[key + "ms"])
            tt("pool", y_ap, ms_ap, neghalf[0:ms_ap.shape[0], :], ALU.pow,
               reads=[key + "ms", "neghalf"], writes=[key + "y"])
            for _ in range(2):
                tt("dve", t_ap, y_ap, y_ap, ALU.mult, reads=[key + "y"], writes=[key + "t"])
                tt("dve", t_ap, t_ap, ms_ap, ALU.mult, reads=[key + "t", key + "ms"], writes=[key + "t"])
                ts("dve", t_ap, t_ap, -0.5, 1.5, ALU.mult, ALU.add, reads=[key + "t"], writes=[key + "t"])
                tt("dve", y_ap, y_ap, t_ap, ALU.mult, reads=[key + "y", key + "t"], writes=[key + "y"])

        def tile_pipeline(items, extra=None):
            rr["nF"] = 4
            rr["F"] = 0
            n = len(items)

            def xk_(i):
                return "xinP%d" % (i % 6)

            def sa(i):
                it = items[i]
                sl, npart = i % 4, it["npart"]
                xt = xinP[i % 6]
                dma("sp", xt[0:npart, :], it["src"], reads=[], writes=[xk_(i)], semkey=xk_(i))
                S.op("pool", lambda e, sl=sl: e.memset(stat[:, sl, 0:1], 0.0), writes=["st%dss" % sl])
                act(sqjunk[0:npart, :], xt[0:npart, :], AF.Square, reads=[xk_(i)], writes=["sqjunk", "st%dss" % sl],
                    accum_out=stat[0:npart, sl, 0:1])

            def sb_(i):
                it = items[i]
                sl, npart = i % 4, it["npart"]
                ts("dve", stat[0:npart, sl, 1:2], stat[0:npart, sl, 0:1], 1.0 / D, EPS, ALU.mult, ALU.add, reads=["st%dss" % sl], writes=["st%dms" % sl])
                tt("pool", stat[0:npart, sl, 2:3], stat[0:npart, sl, 1:2], neghalf[0:npart, :], ALU.pow,
                   reads=["st%dms" % sl, "neghalf"], writes=["st%dy" % sl])

            def sc(i):
                it = items[i]
                sl, npart = i % 4, it["npart"]
                key = "st%d" % sl
                ms_ap, y_ap, t_ap = stat[0:npart, sl, 1:2], stat[0:npart, sl, 2:3], stat[0:npart, sl, 3:4]
                for _ in range(2):
                    tt("dve", t_ap, y_ap, y_ap, ALU.mult, reads=[key + "y"], writes=[key + "t"])
                    tt("dve", t_ap, t_ap, ms_ap, ALU.mult, reads=[key + "t", key + "ms"], writes=[key + "t"])
                    ts("dve", t_ap, t_ap, -0.5, 1.5, ALU.mult, ALU.add, reads=[key + "t"], writes=[key + "t"])
                    tt("dve", y_ap, y_ap, t_ap, ALU.mult, reads=[key + "y", key + "t"], writes=[key + "y"])

            def sd(i):
                it = items[i]
                sl, npart = i % 4, it["npart"]
                act(xsbP[sl][0:npart, :], xinP[i % 6][0:npart, :], AF.Copy, reads=[xk_(i), "st%dy" % sl], writes=["xsbP%d" % sl],
                    scale=stat[0:npart, sl, 2:3])

            def se(i):
                it = items[i]
                sl, npart = i % 4, it["npart"]
                pt, pk = nextT()
                for k in range(8):
                    S.op("pe", lambda e, pt=pt, k=k, sl=sl, npart=npart:
                         e.matmul(pt[:, k * 128:k * 128 + npart], lhsT=xsbP[sl][0:npart, k * 128:(k + 1) * 128], rhs=ident[0:npart, 0:npart], start=True, stop=True),
                         reads=["xsbP%d" % sl, "const"], writes=pk)
                it["dst"](pt, pk)

            def sf(i):
                if items[i].get("post") is not None:
                    items[i]["post"]()

            stages = [sa, sb_, sc, sd, se, sf]
            for i in range(n + len(stages) - 1):
                for d_, fn_ in enumerate(stages):
                    if 0 <= i - d_ < n:
                        fn_(i - d_)
                if extra is not None:
                    extra(i)

        def xn_dst_t(t):
            return lambda pt, pk: xn_dst(pt, pk, t)

        def xn_dst(pt, pk, t):
            src = pt[:, :].rearrange("p (k c) -> p k c", c=128)
            tt("dve", xnT[:, :, 8 + t * 128: 8 + (t + 1) * 128], src,
               colsb[:, 0:8].unsqueeze(2).to_broadcast([128, 8, 128]), ALU.mult,
               reads=pk + ["const"], writes=["xnT_t%d" % t])

        def xn_halo_dst(pt, pk, _=None):
            src = pt[:, :].rearrange("p (k c) -> p k c", c=128)
            nb = colsb[:, 0:8].unsqueeze(2).to_broadcast([128, 8, 8])
            tt("dve", xnT[:, :, 0:8], src[:, :, 0:8], nb, ALU.mult, reads=pk + ["const"], writes=["xnT_hl"])
            tt("dve", xnT[:, :, SEG + 8:SEG + 16], src[:, :, 8:16], nb, ALU.mult, reads=pk + ["const"], writes=["xnT_hr"])

        def mn_dst_fn(mt):
            def f(pt, pk, _=None):
                src = pt[:, :].rearrange("p (k c) -> p k c", c=128)
                tt("dve", mnT[:, :, mt * 128:(mt + 1) * 128], src,
                   colsb[:, 8:16].unsqueeze(2).to_broadcast([128, 8, 128]), ALU.mult, reads=pk + ["const"], writes=["mnT"])
            return f

        xrr = [0]

        def next_x():
            i = xrr[0]
            xrr[0] = 1 - i
            return i

        def fv_post(t, tq, wv, wk, store_fn):
            def f():
                ps, pk = nextF()
                for q in range(2):
                    for k in range(8):
                        mm(ps[:, q * 256:(q + 1) * 256], xnT[:, k, 8 + tq * 128:8 + (tq + 1) * 128], wv[q][:, k, :], k == 0, k == 7,
                           reads=["xnT_t%d" % tq, wk[q]], writes=[pk])
                fs = t % 2
                act(fvt[fs][:, :], ps[:, :], AF.Copy, reads=[pk], writes=["fvt%d" % fs])
                store_fn(t, fvt[fs], "fvt%d" % fs)
            return f

        def pass1(s, fv_dst_fn, aux_only=False):
            wv = wk = None
            if fv_dst_fn is not None:
                nat = [load_w(w_cols(wb_in, 2048 + q * 256, 256)[0], (8, 256)) for q in range(2)]
                wv, wk = [], []
                for qp in range(2):
                    i = rr["w"]
                    rr["w"] = (i + 1) % NSLOT
                    pv_ = wslot[i][:, 0:2048].rearrange("p (k c) -> p k c", c=256)
                    pkey = "wslot%d" % i
                    for sq in range(2):
                        o_ = pv_.rearrange("p k (c g) -> p k c g", g=64)[:, :, :, sq * 32:(sq + 1) * 32]
                        i_ = nat[sq][0].rearrange("p k (g c) -> p k c g", c=8)[:, :, 4 * qp:4 * qp + 4, :]
                        cp("pool", o_, i_, reads=[nat[sq][1]], writes=[pkey])
                    wv.append(pv_)
                    wk.append(pkey)
            items = [dict(src=xh[s, :, :], npart=16, dst=xn_halo_dst, post=None)]
            for t in range(NT if not aux_only else 0):
                items.append(dict(src=xs[s, t * 128:(t + 1) * 128, :], npart=128, dst=xn_dst_t(t),
                                  post=(fv_post(t, t, wv, wk, fv_dst_fn) if fv_dst_fn is not None else None)))
            for mt in range(2):
                items.append(dict(src=mem[s, mt * 128:(mt + 1) * 128, :], npart=128, dst=mn_dst_fn(mt), post=None))
            tile_pipeline(items)
            for hp in range(2):
                wv, wk = load_w(w_cols(wb_kv, hp * 256, 256)[0], (8, 256))
                for hh in range(2):
                    h = 2 * hp + hh
                    ps, pk = nextF()
                    for k in range(8):
                        mm(ps[:, 0:256], wv[:, k, hh * 128:(hh + 1) * 128], mnT[:, k, :], k == 0, k == 7, reads=["mnT", wk], writes=[pk])
                    act(KT[:, h, :], ps[:, 0:256], AF.Copy, reads=[pk], writes=["KT"])
            wq_ = [load_w(w_cols(wb_kv, 512 + q * 256, 256)[0], (8, 256)) for q in range(2)]
            for mt in range(2):
                ps, pk = nextF()
                for q in range(2):
                    for k in range(8):
                        mm(ps[:, q * 256:(q + 1) * 256], mnT[:, k, mt * 128:(mt + 1) * 128], wq_[q][0][:, k, :], k == 0, k == 7,
                           reads=["mnT", wq_[q][1]], writes=[pk])
                act(Vv[:, mt, :], ps[:, :], AF.Copy, reads=[pk], writes=["Vv"])

        def fourier_groups(v, lhs_fn, out_fn, kbr=None, nB=256, dkey="Dt"):
            rr["nF"] = 4
            rr["F"] = 0
            if kbr is None:
                kbr = (kb[:, v, 0, :], kb[:, v, 1, :])
            pa = {}

            def fa(g):
                ps, pk = nextF()
                mm(ps[:, 0:256], lhs_fn(g), dfta[:, :], True, True, reads=[dkey, "const"], writes=[pk])
                pa[g] = (ps, pk)

            def ft1(g):
                ps, pk = pa[g]
                A2 = ps[:, 0:256].rearrange("p (a b) -> p a b", a=2)
                w4, k4 = tw4[g % 3], "tw4_%d" % (g % 3)
                tt("dve", w4[:, 0:2, :], A2, twid[:, v, 0, :, :], ALU.mult, reads=[pk, "const"], writes=[k4 + "a"])
                tt("dve", w4[:, 2:4, :], A2, twid[:, v, 1, :, :], ALU.mult, reads=[pk, "const"], writes=[k4 + "c"])

            def ft2(g):
                w4, k4 = tw4[g % 3], "tw4_%d" % (g % 3)
                yk = "Yt%d" % (g % 3)
                w4p = w4[:, :, :].rearrange("p (a b) c -> p a b c", b=2)
                tt("pool", Yt[g % 3][:, :, :], w4p[:, :, 0, :], w4p[:, :, 1, :], ALU.add, reads=[k4 + "a", k4 + "c"], writes=[yk + "r", yk + "i"])

            def fb(g):
                yk = "Yt%d" % (g % 3)
                ps2, pk2 = nextF()
                mm(ps2[:, 0:nB], Yt[g % 3][:, 0, :], kbr[0], True, False, reads=[yk + "r", "const"], writes=[pk2])
                mm(ps2[:, 0:nB], Yt[g % 3][:, 1, :], kbr[1], False, True, reads=[yk + "i", "const"], writes=[pk2])
                out_fn(g, ps2, pk2)

            for i in range(64 + 3):
                if i < 64:
                    fa(i)
                if 0 <= i - 1 < 64:
                    ft1(i - 1)
                if 0 <= i - 2 < 64:
                    ft2(i - 2)
                if 0 <= i - 3 < 64:
                    fb(i - 3)

        def f_finish(v, tile_fn, t0):
            for tl in range(4):
                hap, fkeys = tile_fn(t0 + tl)
                pt, pk = nextT()
                for h in range(4):
                    for ri in range(2):
                        for (iap, p0, npp) in hap(ri, h):
                            S.op("pe", lambda e, pt=pt, h=h, ri=ri, iap=iap, p0=p0, npp=npp:
                                 e.matmul(pt[p0:p0 + npp, (h * 2 + ri) * 128:(h * 2 + ri + 1) * 128], lhsT=iap, rhs=ident[:, :], start=True, stop=True),
                                 reads=fkeys + ["const"], writes=pk)
                cp(("dve", "act")[tl % 2], FTt[:, :, tl * 128:(tl + 1) * 128], pt[:, :].rearrange("p (a c) -> p a c", c=128), reads=pk, writes=["FTt%d" % tl])
            for h in range(4):
                ps, pk = nextF()
                mm(ps[:, :], fold[:, v, h, 0, :], FTt[:, h * 2, :], True, False, reads=["fold"] + ["FTt%d" % i for i in range(4)], writes=[pk])
                mm(ps[:, :], fold[:, v, h, 1, :], FTt[:, h * 2 + 1, :], False, True, reads=["fold"] + ["FTt%d" % i for i in range(4)], writes=[pk])
                cp("act", GT[:, h, t0 * 128:(t0 + 4) * 128], ps[:, :], reads=[pk], writes=["GT_%d" % (t0 // 4)])

        Dv16 = Dt[:, :].rearrange("p (i c) -> p i c", c=512)
        Dv128 = Dt[:, :].rearrange("p (i c) -> p i c", c=64)
        Fp = Fsb[:, :].rearrange("p (r k c) -> p r k c", r=2, k=16)
        Fs = Fsb[:, :].rearrange("p (r k c) -> p r k c", r=2, k=128)

        def fourier_prompt(s):
            dma("sp", Dt[:, :], fvS[s, :, :].rearrange("(p i) c -> p (i c)", i=16), reads=["fvS%d_%d" % (s, t) for t in range(NT)], writes=["Dt"], semkey="Dt")

            def out_fn(g, ps2, pk2):
                cp("act", Fp[:, :, :, 8 * g:8 * g + 8], ps2[:, 0:256].rearrange("p (r k c) -> p r k c", r=2, k=16), reads=[pk2], writes=["Fsb"])

            fourier_groups(0, lambda g: Dt[:, :].rearrange("p (m g) -> p m g", g=64)[:, :, g], out_fn)
            for t0 in range(0, NT, 4):
                f_finish(0, lambda t: ((lambda ri, h, t=t: [(Fp[:, ri, t, h * 128:(h + 1) * 128], 0, 128)]), ["Fsb"]), t0)

        def sample_fv_all():
            wq_ = [load_w(w_cols(wb_in, 2048 + q * 256, 256)[0], (8, 256), dep="wcast_fv") for q in range(2)]
            wv, wk = [x_[0] for x_ in wq_], [x_[1] for x_ in wq_]

            def store(t, tile, key):
                dma("act", G1[t * 128:(t + 1) * 128, :], tile[:, :], reads=[key], writes=["G1_%d" % t], semkey="fvst%d" % (t % 2))

            items = []
            for t in range(128):
                tq = t % NT
                items.append(dict(src=xfull[t * 128:(t + 1) * 128, :], npart=128, dst=xn_dst_t(tq), post=fv_post(t, tq, wv, wk, store)))
            tile_pipeline(items, extra=lambda i: issue_casts(1) if (i >= 8 and i % 2 == 0) else None)
            issue_casts(1000)
            dma("sp", wpg[:, :, :], wb_pg[:, :].rearrange("(k p) c -> p k c", p=128), reads=["wcast"], writes=["wpg"], semkey="wpg")

        def fourier_sample_local():
            Fp64 = Fp[:, :, :, :].rearrange("p r k (b c) -> p r k b c", c=64)
            Dbuf = [(Dt, "Dt"), (Dt2, "Dt2")]

            def load_blk(b):
                db, dk = Dbuf[b % 2]
                g1v = G1[:, :].rearrange("(p i) c -> p i c", i=128)
                dbv = db[:, :].rearrange("p (i c) -> p i c", c=64)
                for i0 in range(0, 128, 32):
                    dma("sp", dbv[:, i0:i0 + 32, :], g1v[:, i0:i0 + 32, b * 64:(b + 1) * 64],
                        reads=["G1_%d" % t for t in range(128)], writes=[dk], semkey=dk)

            load_blk(0)
            for b in range(8):
                if b + 1 < 8:
                    load_blk(b + 1)
                db, dk = Dbuf[b % 2]
                dv = db[:, :].rearrange("p (i c) -> p i c", c=64)

                def out_fn(g, ps2, pk2, b=b):
                    cp("act", Fp[:, :, :, 64 * b + g], ps2[:, 0:32].rearrange("p (r k) -> p r k", r=2), reads=[pk2], writes=["Fsb"])

                fourier_groups(1, lambda g, dv=dv: dv[:, :, g], out_fn, kbr=(kbo[:, 0, :], kbo[:, 1, :]), nB=32, dkey=dk)
            S.fence()
            for t0 in range(0, NT, 4):
                f_finish(1, lambda t: ((lambda ri, h, t=t: [(Fp[:, ri, t, h * 128:(h + 1) * 128], 0, 128)]), ["Fsb"]), t0)

        WIN_OFF = {"pv": 0, "pg": 1024, "fv": 2048, "fg": 2560, "q": 3072, "ag": 3584, "mg": 4096}

        wpref = {}

        def proj_fm(col0, nchunk, c0, evac_fn, xkeys):
            done = 0
            while done < nchunk:
                n = min(2, nchunk - done)
                if (col0 + done * 128) in wpref:
                    wv, wk = wpref.pop(col0 + done * 128)
                else:
                    wv, wk = load_w(w_cols(wb_in, col0 + done * 128, n * 128)[0], (8, n * 128))
                for j in range(n):
                    ps, pk = nextF()
                    for k in range(8):
                        mm(ps[:, :], wv[:, k, j * 128:(j + 1) * 128], xnT[:, k, c0:c0 + CH], k == 0, k == 7, reads=xkeys + [wk], writes=[pk])
                    evac_fn(done + j, ps, pk, wv, wk)
                done += n

        wo_pref = [None]

        def pass2_chunk(s, j, stage):
            c0 = 8 + j * CH
            xkeys = ["xnT_t%d" % t for t in range(4 * j, 4 * j + 4)]
            hkeys = xkeys + ["xnT_hl", "xnT_hr"] + (["xnT_t%d" % (4 * j - 1)] if j > 0 else []) + (["xnT_t%d" % (4 * j + 4)] if j < 3 else [])
            SCL = 1.0 / math.sqrt(128.0)
            if stage == "P":
                if j > 0:
                    cp("act", pvh[:, :, 0:16], pvsave[:, :, :], reads=["pvsave"], writes=["pvh%d" % i_ for i_ in range(8)])

                def pv_evac(i, ps, pk, wv, wk):
                    cp("act", pvh[:, i, 16:16 + CH], ps[:, :], reads=[pk], writes=["pvh%d" % i])
                    if j == 0:
                        ph, phk = nextF()
                        for k in range(8):
                            mm(ph[:, 0:16], wv[:, k, (i % 2) * 128:(i % 2 + 1) * 128], xnT[:, k, c0 - 8:c0 + 8], k == 0, k == 7,
                               reads=hkeys + [wk], writes=[phk])
                        cp("act", pvh[:, i, 0:16], ph[:, 0:16], reads=[phk], writes=["pvh%d" % i])

                proj_fm(WIN_OFF["pv"], 8, c0 + 8, pv_evac, hkeys)
                if j < 3:
                    cp("act", pvsave[:, :, :], pvh[:, :, CH:CH + 16], reads=["pvh%d" % i_ for i_ in range(8)], writes=["pvsave"])
                proj_fm(WIN_OFF["pg"], 8, c0, lambda i, ps, pk, wv, wk: act(pgs[:, i, :], ps[:, :], AF.Silu, reads=[pk], writes=["pgs%d" % i]), xkeys)
                W = CH + 16
                for g, w in enumerate((2, 4, 8, 16)):
                    u = pvh[:, 2 * g:2 * g + 2, :]
                    uk = ["pvh%d" % (2 * g), "pvh%d" % (2 * g + 1)]
                    tt("dve", tA[:, :, 1:W], u[:, :, 1:W], u[:, :, 0:W - 1], ALU.add, reads=uk, writes=["tA"])
                    fin, fk = tA, "tA"
                    if w >= 4:
                        if w == 4:
                            tt("dve", tB[:, :, 8:8 + CH], tA[:, :, 9:9 + CH], tA[:, :, 7:7 + CH], ALU.add, reads=["tA"], writes=["tB"])
                            fin, fk = tB, "tB"
                        else:
                            tt("dve", tB[:, :, 3:W], tA[:, :, 3:W], tA[:, :, 1:W - 2], ALU.add, reads=["tA"], writes=["tB"])
                            if w == 8:
                                tt("dve", tC[:, :, 8:8 + CH], tB[:, :, 11:11 + CH], tB[:, :, 7:7 + CH], ALU.add, reads=["tB"], writes=["tC"])
                                fin, fk = tC, "tC"
                            else:
                                tt("dve", tC[:, :, 7:W], tB[:, :, 7:W], tB[:, :, 3:W - 4], ALU.add, reads=["tB"], writes=["tC"])
                                tt("dve", tA[:, :, 8:8 + CH], tC[:, :, 15:15 + CH], tC[:, :, 7:7 + CH], ALU.add, reads=["tC"], writes=["tA"])
                                fin, fk = tA, "tA"
                    pk_ = ["pT%d" % (2 * g), "pT%d" % (2 * g + 1)]
                    stt("dve", pT[:, 2 * g:2 * g + 2, :], fin[:, :, 8:8 + CH], 1.0 / w, u[:, :, 8:8 + CH], ALU.mult, ALU.subtract, reads=[fk] + uk, writes=pk_)
                    for edge, (m0, e0) in enumerate(((8, 0), (8 + CH - 8, 8))):
                        if (edge == 0 and j == 0) or (edge == 1 and j == 3):
                            ic = icn[:, s, g, e0:e0 + 8].unsqueeze(1).to_broadcast([128, 2, 8])
                            tt("dve", fin[:, :, m0:m0 + 8], fin[:, :, m0:m0 + 8], ic, ALU.mult, reads=[fk, "const"], writes=[fk])
                            tt("dve", pT[:, 2 * g:2 * g + 2, m0 - 8:m0], fin[:, :, m0:m0 + 8], u[:, :, m0:m0 + 8], ALU.subtract, reads=[fk] + uk, writes=pk_)
            if stage == "G":
                for g in range(4):
                    pk_ = ["pT%d" % (2 * g), "pT%d" % (2 * g + 1)]
                    for mo in range(2):
                        ps, pk = nextF()
                        for ki in range(2):
                            mm(ps[:, :], wpg[:, 2 * g + ki, mo * 128:(mo + 1) * 128], pT[:, 2 * g + ki, :], ki == 0, ki == 1, reads=pk_ + ["wpg"], writes=[pk])
                        ci = 2 * g + mo
                        stt("dve", gaT[:, ci, :], ps[:, :], NCOL(16 + ci), pgs[:, ci, :], ALU.mult, ALU.mult, reads=[pk, "const", "pgs%d" % ci], writes=["gaT"])
            if stage == "A":
                proj_fm(WIN_OFF["q"], 4, c0, lambda i, ps, pk, wv, wk: act(qT[:, i, :], ps[:, :], AF.Copy, reads=[pk], writes=["qT"]), xkeys)
                proj_fm(WIN_OFF["ag"], 4, c0, lambda i, ps, pk, wv, wk: act(ags[:, i, :], ps[:, :], AF.Silu, reads=[pk], writes=["ags"]), xkeys)
                def att_scores(h):
                    es = h % 2
                    ek = "E%d" % es
                    for mt in range(2):
                        ps, pk = nextF()
                        mm(ps[:, :], KT[:, h, mt * 128:(mt + 1) * 128], qT[:, h, :], True, True, reads=["KT", "qT"], writes=[pk])
                        act(Ee[es][:, mt, :], ps[:, :], AF.Exp, reads=[pk], writes=[ek + "_%d" % mt], scale=SCL)

                def att_pv(h):
                    es = h % 2
                    ek = "E%d" % es
                    po, pok = nextF()
                    for mt in range(2):
                        mm(po[:, :], Vv[:, mt, h * 128:(h + 1) * 128], Ee[es][:, mt, :], mt == 0, mt == 1, reads=["Vv", ek + "_%d" % mt], writes=[pok])
                    pd, pdk = nextF()
                    for mt in range(2):
                        mm(pd[:, :], ones[:, :], Ee[es][:, mt, :], mt == 0, mt == 1, reads=["ones", ek + "_%d" % mt], writes=[pdk])
                    S.op("dve", lambda e, pd=pd: e.reciprocal(out=rden[:, :], in_=pd[:, :]), reads=[pdk], writes=["rden"])
                    tt("dve", otmp[:, :], po[:, :], rden[:, :], ALU.mult, reads=[pok, "rden"], writes=["otmp"])
                    tt("pool", gcT[:, h, :], otmp[:, :], ags[:, h, :], ALU.mult, reads=["otmp", "ags"], writes=["gcT"])

                att_scores(0)
                for h in range(4):
                    if h + 1 < 4:
                        att_scores(h + 1)
                    att_pv(h)
                def fg_evac(i, ps, pk, wv, wk):
                    act(fgs[:, i, :], ps[:, :], AF.Silu, reads=[pk], writes=["fgs%d" % i])
                    tt("dve", gbT[:, i, :], GT[:, i, j * CH:(j + 1) * CH], fgs[:, i, :], ALU.mult, reads=["GT_%d" % j, "fgs%d" % i], writes=["gbT"])

                proj_fm(WIN_OFF["fg"], 4, c0, fg_evac, xkeys)
            if stage == "M":
                for mp in range(4):
                    wg = [load_w(w_cols(wb_in, 4096 + b * 1024 + mp * 256, 256)[0], (8, 256)) for b in range(3)]
                    wpo_, kpo = load_w(w_cols(wb_po, mp * 256, 256)[0], (8, 256))
                    i = rr["w"]
                    rr["w"] = (i + 1) % NSLOT
                    wfa = wslot[i][:, 0:2048].rearrange("p (k c) -> p k c", c=256)
                    kfa = "wslot%d" % i
                    dma("sp", wfa[:, 0:4, :], wb_fo[mp, :, :].rearrange("p (k c) -> p k c", k=4), reads=["wcast"], writes=[kfa], semkey=kfa)
                    dma("sp", wfa[:, 4:8, :], wb_ao[mp, :, :].rearrange("p (k c) -> p k c", k=4), reads=["wcast"], writes=[kfa], semkey=kfa)
                    for mm_ in range(2):
                        m = 2 * mp + mm_
                        cs = slice(mm_ * 128, (mm_ + 1) * 128)
                        gs_ = m % 2
                        for b in range(3):
                            wg_, wgk = wg[b]
                            ps, pk = nextF()
                            for k in range(8):
                                mm(ps[:, :], wg_[:, k, cs], xnT[:, k, c0:c0 + CH], k == 0, k == 7, reads=xkeys + [wgk], writes=[pk])
                            act(gate[gs_][:, b, :], ps[:, :], AF.Sigmoid, reads=[pk, "const"], writes=["gate%d_%d" % (gs_, b)], bias=NCOL(24 + b * 8 + m))
                        pa, pak = nextF()
                        for k in range(8):
                            mm(pa[:, :], wpo_[:, k, cs], gaT[:, k, :], k == 0, k == 7, reads=["gaT", kpo], writes=[pak])
                        tt("dve", m1[:, :], pa[:, :], gate[gs_][:, 0, :], ALU.mult, reads=[pak, "gate%d_0" % gs_], writes=["m1"])
                        pb, pbk = nextF()
                        for k in range(4):
                            mm(pb[:, :], wfa[:, k, cs], gbT[:, k, :], k == 0, k == 3, reads=["gbT", kfa], writes=[pbk])
                        tt("dve", m2[:, :], pb[:, :], gate[gs_][:, 1, :], ALU.mult, reads=[pbk, "gate%d_1" % gs_], writes=["m2"])
                        pc, pck = nextF()
                        for k in range(4):
                            mm(pc[:, :], wfa[:, 4 + k, cs], gcT[:, k, :], k == 0, k == 3, reads=["gcT", kfa], writes=[pck])
                        tt("dve", m3[:, :], pc[:, :], gate[gs_][:, 2, :], ALU.mult, reads=[pck, "gate%d_2" % gs_], writes=["m3"])
                        tt("pool", m1[:, :], m1[:, :], m2[:, :], ALU.add, reads=["m1", "m2"], writes=["m1"])
                        tt("pool", mT[:, m, :], m1[:, :], m3[:, :], ALU.add, reads=["m1", "m3"], writes=["mT"])
                wo_pref[0] = [load_w(w_cols(wb_o, q * 256, 256)[0], (8, 256)) for q in range(4)]
            if stage == "O":
                wo = wo_pref[0]
                info = {}
                if j + 1 < 4:
                    for col_ in (WIN_OFF["q"], WIN_OFF["q"] + 256, WIN_OFF["ag"], WIN_OFF["ag"] + 256):
                        wpref[col_] = load_w(w_cols(wb_in, col_, 256)[0], (8, 256))

                def o_main(tl):
                    t = 4 * j + tl
                    sl = next_x()
                    xk = "xin%d" % sl
                    dma("sp", xin[sl][:, :], xs[s, t * 128:(t + 1) * 128, :], reads=[], writes=[xk], semkey=xk)
                    hs = t % 2
                    hk = "hout%d" % hs
                    S.op("pool", lambda e, hs=hs: e.memset(stat[:, 4 + hs, 0:2], 0.0), writes=["sf%dssa" % hs])
                    for hf in range(2):
                        ps, pk = nextF()
                        for qq in range(2):
                            wv, wk = wo[2 * hf + qq]
                            for k in range(8):
                                mm(ps[:, qq * 256:(qq + 1) * 256], mT[:, k, tl * 128:(tl + 1) * 128], wv[:, k, :], k == 0, k == 7, reads=["mT", wk], writes=[pk])
                        tt("dve", hout[hs][:, hf * 512:(hf + 1) * 512], ps[:, :], xin[sl][:, hf * 512:(hf + 1) * 512], ALU.add, reads=[pk, xk], writes=[hk])
                        act(xin[sl][:, hf * 512:(hf + 1) * 512], hout[hs][:, hf * 512:(hf + 1) * 512], AF.Square, reads=[hk], writes=[xk, "sf%dssa" % hs],
                            accum_out=stat[:, 4 + hs, hf:hf + 1])
                    tt("dve", stat[:, 4 + hs, 2:3], stat[:, 4 + hs, 0:1], stat[:, 4 + hs, 1:2], ALU.add, reads=["sf%dssa" % hs], writes=["sf%dss" % hs])
                    ts("dve", stat[:, 4 + hs, 3:4], stat[:, 4 + hs, 2:3], 1.0 / D, EPS, ALU.mult, ALU.add, reads=["sf%dss" % hs], writes=["sf%dms" % hs])
                    tt("pool", stat[:, 4 + hs, 4:5], stat[:, 4 + hs, 3:4], neghalf[:, :], ALU.pow, reads=["sf%dms" % hs, "neghalf"], writes=["sf%dy" % hs])

                def o_fin(tl):
                    t = 4 * j + tl
                    hs = t % 2
                    hk = "hout%d" % hs
                    key = "sf%d" % hs
                    ms_ap, y_ap, t_ap = stat[:, 4 + hs, 3:4], stat[:, 4 + hs, 4:5], stat[:, 4 + hs, 5:6]
                    for _ in range(2):
                        tt("dve", t_ap, y_ap, y_ap, ALU.mult, reads=[key + "y"], writes=[key + "t"])
                        tt("dve", t_ap, t_ap, ms_ap, ALU.mult, reads=[key + "t", key + "ms"], writes=[key + "t"])
                        ts("dve", t_ap, t_ap, -0.5, 1.5, ALU.mult, ALU.add, reads=[key + "t"], writes=[key + "t"])
                        tt("dve", y_ap, y_ap, t_ap, ALU.mult, reads=[key + "y", key + "t"], writes=[key + "y"])
                    stt("dve", hout[hs][:, :], hout[hs][:, :], y_ap, nfr[:, :], ALU.mult, ALU.mult, reads=[hk, key + "y", "const"], writes=[hk])
                    dma("act", yout[s, t * 128:(t + 1) * 128, :], hout[hs][:, :], reads=[hk], writes=[], semkey="yst%d" % hs)

                for tl in range(5):
                    if tl < 4:
                        o_main(tl)
                    if tl >= 1:
                        o_fin(tl - 1)

        def fv_to_scratch(s):
            def f(t, tile, key):
                dma("act", fvS[s, t * 128:(t + 1) * 128, :], tile[:, :], reads=[key], writes=["fvS%d_%d" % (s, t)], semkey="fvst%d" % (t % 2))
            return f

        def pass2_segment(s):
            rr["nF"] = 8
            S.rgn_keys = set(["pvh%d" % i for i in range(8)] + ["tA", "tB", "tC", "qT", "ags", "rden", "otmp"]
                             + ["E%d_%d" % (a_, b_) for a_ in range(2) for b_ in range(2)] + ["fgs%d" % i for i in range(4)])
            marker = lambda: S.op("pool", lambda e: e.memset(stat[:, 7, 0:1], 0.0), writes=["RGN"])
            marker()
            pass2_chunk(s, 0, "P")
            pass2_chunk(s, 0, "G")
            for j in range(4):
                marker()
                pass2_chunk(s, j, "A")
                marker()
                if j + 1 < 4:
                    pass2_chunk(s, j + 1, "P")
                pass2_chunk(s, j, "M")
                if j + 1 < 4:
                    pass2_chunk(s, j + 1, "G")
                pass2_chunk(s, j, "O")
            S.rgn_keys = set()

        sample_fv_all()
        S.fence()
        fourier_sample_local()
        S.fence()
        pass1(2, None, aux_only=True)
        S.fence()
        pass2_segment(2)
        S.fence()
        for s_ in range(2):
            pass1(s_, fv_to_scratch(s_))
            S.fence()
            fourier_prompt(s_)
            S.fence()
            pass2_segment(s_)
            S.fence()

        S.finalize()
        S.emit({"act": ["yst0", "yst1"]})
    return nc


def _tables(r=0):
    bf = ml_dtypes.bfloat16
    p = np.arange(128, dtype=np.float64)
    ang = 2 * np.pi * np.outer(p, p) / 128.0
    C, Sn = np.cos(ang), np.sin(ang)
    dfta = np.concatenate([C, -Sn], axis=1).astype(np.float32).astype(bf)
    k = np.arange(128, dtype=np.float64)
    tw = np.zeros((128, 2, 2, 2, 128), np.float64)
    i_p = (np.arange(128) // 8).astype(np.float64)
    rot = 2 * np.pi * np.outer(np.ones(128), k) * (r + 1) / 8.0
    for v_, a_ in enumerate((2 * np.pi * np.outer(i_p, k) / 2048.0, 2 * np.pi * np.outer(p, k) / 16384.0 + rot)):
        tw[:, v_, 0, 0], tw[:, v_, 0, 1] = np.cos(a_), np.sin(a_)
        tw[:, v_, 1, 0], tw[:, v_, 1, 1] = -np.sin(a_), np.cos(a_)
    kbm = np.zeros((128, 2, 2, 256), np.float64)
    i16 = np.arange(16, dtype=np.float64)
    a = 2 * np.pi * np.outer(i16, i16) / 16.0
    kc = np.kron(np.cos(a), np.eye(8))
    ks = np.kron(-np.sin(a), np.eye(8))
    kbm[:, 0, 0] = np.concatenate([kc, ks], axis=1)
    kbm[:, 0, 1] = np.concatenate([-ks, kc], axis=1)
    kbm[:, 1, 0] = np.concatenate([C, -Sn], axis=1)
    kbm[:, 1, 1] = np.concatenate([Sn, C], axis=1)
    cs = np.concatenate([C, Sn], axis=1).astype(np.float32)
    return {
        "t_ident": np.eye(128, dtype=np.float32).astype(bf),
        "t_dfta": dfta,
        "t_twid": tw.reshape(128, -1).astype(np.float32),
        "t_kb": kbm.reshape(128, -1).astype(np.float32).astype(bf),
        "t_cs": cs,
    }


def _kbo(r):
    i = np.arange(128, dtype=np.float64)
    kh = np.arange(16 * r, 16 * r + 16, dtype=np.float64)
    a = 2 * np.pi * np.outer(i, kh) / 128.0
    kc, ks = np.cos(a), -np.sin(a)
    m = np.concatenate([kc, ks, -ks, kc], axis=1)
    return m.astype(np.float32).astype(ml_dtypes.bfloat16)


def _invcnt():
    out = np.zeros((NCORES, 3, 4, 16), np.float32)
    for r in range(NCORES):
        for s in range(3):
            if s < 2:
                S_, base = SEG, 0
            else:
                S_, base = 16384, SEG * r
            for g, w in enumerate((2, 4, 8, 16)):
                half = w // 2
                for e in range(16):
                    t = base + (e if e < 8 else SEG - 16 + e)
                    lo = min(max(t - half, 0), S_)
                    hi = min(max(t + half, 0), S_)
                    out[r, s, g, e] = 1.0 / float(hi - lo)
    return out


_NC_CACHE = {}


def kernel(x_prompt, x_sample, mem_prompt, mem_sample, norm_in, norm_mem, w_in, w_pool_grp,
           pool_scale, w_four_grp, w_kv, w_pool_out, w_four_out, w_attn_out, b_gate, w_o, norm_f):
    f = lambda a: np.ascontiguousarray(np.asarray(a), dtype=np.float32)
    x_prompt, x_sample, mem_prompt, mem_sample = f(x_prompt), f(x_sample), f(mem_prompt), f(mem_sample)
    tabs = _tables()
    tabs.pop("t_twid")
    icnt = _invcnt()
    cols = np.concatenate([
        f(norm_in)[0].reshape(8, 128).T, f(norm_mem)[0].reshape(8, 128).T,
        f(pool_scale)[0].reshape(8, 128).T, f(b_gate)[0].reshape(24, 128).T], axis=1)
    shared = {
        "w_in": f(w_in)[0], "w_kv": f(w_kv)[0], "w_po": f(w_pool_out)[0], "w_fo": f(w_four_out)[0],
        "w_ao": f(w_attn_out)[0], "w_o": f(w_o)[0], "w_pg": f(w_pool_grp)[0].reshape(1024, 256),
        "w_fg": np.ascontiguousarray(f(w_four_grp)[0].transpose(1, 0, 2)),
        "cols": np.ascontiguousarray(cols), "nf_row": f(norm_f).reshape(1, D),
    }
    shared.update(tabs)
    in_maps = []
    zeros8 = np.zeros((8, D), np.float32)
    for r in range(NCORES):
        xs = np.stack([x_prompt[2 * r], x_prompt[2 * r + 1], x_sample[0, SEG * r:SEG * (r + 1)]])
        xh = np.zeros((3, 16, D), np.float32)
        xh[2, 0:8] = x_sample[0, SEG * r - 8:SEG * r] if r > 0 else zeros8
        xh[2, 8:16] = x_sample[0, SEG * (r + 1):SEG * (r + 1) + 8] if r < NCORES - 1 else zeros8
        mem = np.stack([mem_prompt[2 * r], mem_prompt[2 * r + 1], mem_sample[0]])
        m = {"xs": xs, "xh": xh, "mem": mem, "icnt": icnt[r].reshape(1, -1), "t_kbo": _kbo(r),
             "xfull": np.ascontiguousarray(np.roll(x_sample[0], -SEG * (r + 1), axis=0)), "t_twid": _tables(r)["t_twid"]}
        m.update(shared)
        in_maps.append(m)
    if "nc" not in _NC_CACHE:
        _NC_CACHE["nc"] = build_program()
    res = run_bass_kernel_spmd(_NC_CACHE["nc"], in_maps, core_ids=list(range(NCORES)))
    y_prompt = np.empty((16, SEG, D), np.float32)
    y_sample = np.empty((1, 16384, D), np.float32)
    for r in range(NCORES):
        yo = res.results[r]["yout"]
        y_prompt[2 * r] = yo[0]
        y_prompt[2 * r + 1] = yo[1]
        y_sample[0, SEG * r:SEG * (r + 1)] = yo[2]
    return (y_prompt, y_sample)
```
